# Optimizing a Trainium2 kernel written in Bass

```python
import math
import jax, jax.numpy as jnp
from jax import lax
import numpy as np

D_MODEL = 1024
BATCH = 4
SEQ = 8192
DEPTH = 2

CHUNK = 64
Q_BLOCK = 128
HEAD_DIM = 64
H_FOX = 8
H_CHK = 8
W_FOX = H_FOX * HEAD_DIM
W_CHK = H_CHK * HEAD_DIM
LEFT_CHUNKS = 8
MAX_REL = 128
N_REL = 2 * MAX_REL + 1
N_BRANCH = 2
D_FF = 4 * D_MODEL
EPS = 1e-6
NEG = -1e30
N_IN = 3 * W_FOX + H_FOX + 3 * W_CHK + N_BRANCH * D_MODEL
SPLITS = tuple(np.cumsum([W_FOX, W_FOX, W_FOX, H_FOX, W_CHK, W_CHK, W_CHK]).tolist())

kernel_name = "hybrid_fox_chunkattn_gated_block"


def rms_norm(x, g):
    xf = x.astype(jnp.float32)
    y = xf * lax.rsqrt(jnp.mean(xf * xf, axis=-1, keepdims=True) + EPS)
    return (y * g.astype(jnp.float32)).astype(x.dtype)


def to_heads(t, h):
    b, s, _ = t.shape
    return t.reshape(b, s, h, HEAD_DIM).transpose(0, 2, 1, 3)


def from_heads(t):
    b, h, s, d = t.shape
    return t.transpose(0, 2, 1, 3).reshape(b, s, h * d)


def forgetting_attention(q, k, v, log_f):
    b, h, s, d = q.shape
    nblk = s // Q_BLOCK
    scale = 1.0 / math.sqrt(d)
    c = jnp.cumsum(log_f, axis=-1)
    qb = q.reshape(b, h, nblk, Q_BLOCK, d).transpose(2, 0, 1, 3, 4)
    cb = c.reshape(b, h, nblk, Q_BLOCK).transpose(2, 0, 1, 3)
    key_pos = jnp.arange(s, dtype=jnp.int32)

    def block(args):
        qi, ci, i = args
        sc = jnp.einsum('bhqd,bhkd->bhqk', qi, k, preferred_element_type=jnp.float32)
        sc = sc * scale + ci[..., None] - c[:, :, None, :]
        tq = i * Q_BLOCK + jnp.arange(Q_BLOCK, dtype=jnp.int32)
        mask = key_pos[None, :] <= tq[:, None]
        sc = jnp.where(mask, sc, NEG)
        p = jax.nn.softmax(sc, axis=-1).astype(v.dtype)
        return jnp.einsum('bhqk,bhkd->bhqd', p, v)

    out = lax.map(block, (qb, cb, jnp.arange(nblk, dtype=jnp.int32)))
    return out.transpose(1, 2, 0, 3, 4).reshape(b, h, s, d)


def chunk_band_attention(q, k, v, rel_bias):
    b, h, s, d = q.shape
    nblk = s // Q_BLOCK
    scale = 1.0 / math.sqrt(d)
    pad = LEFT_CHUNKS * CHUNK
    band = pad + Q_BLOCK
    kp = jnp.pad(k, ((0, 0), (0, 0), (pad, 0), (0, 0)))
    vp = jnp.pad(v, ((0, 0), (0, 0), (pad, 0), (0, 0)))
    qb = q.reshape(b, h, nblk, Q_BLOCK, d).transpose(2, 0, 1, 3, 4)

    def block(args):
        qi, i = args
        p0 = i * Q_BLOCK
        kb = lax.dynamic_slice_in_dim(kp, p0, band, axis=2)
        vb = lax.dynamic_slice_in_dim(vp, p0, band, axis=2)
        tq = p0 + jnp.arange(Q_BLOCK, dtype=jnp.int32)
        sk = p0 - pad + jnp.arange(band, dtype=jnp.int32)
        cq = tq // CHUNK
        ck = sk // CHUNK
        valid = (sk[None, :] >= 0) & (ck[None, :] <= cq[:, None]) & (ck[None, :] >= cq[:, None] - LEFT_CHUNKS)
        rel = jnp.clip(tq[:, None] - sk[None, :], -MAX_REL, MAX_REL) + MAX_REL
        bias = rel_bias[:, rel].astype(jnp.float32)
        sc = jnp.einsum('bhqd,bhkd->bhqk', qi, kb, preferred_element_type=jnp.float32)
        sc = jnp.where(valid, sc * scale + bias[None], NEG)
        p = jax.nn.softmax(sc, axis=-1).astype(vb.dtype)
        return jnp.einsum('bhqk,bhkd->bhqd', p, vb)

    out = lax.map(block, (qb, jnp.arange(nblk, dtype=jnp.int32)))
    return out.transpose(1, 2, 0, 3, 4).reshape(b, h, s, d)


def setup_inputs(seed: int = 0) -> dict:
    key = jax.random.key(seed)
    ks = jax.random.split(key, 14)
    nrm = lambda k, shape, fan_in: jax.random.normal(k, shape, jnp.float32) * (fan_in ** -0.5)
    x = jax.random.normal(ks[0], (BATCH, SEQ, D_MODEL), jnp.float32)
    norm1 = 1.0 + 0.02 * jax.random.normal(ks[1], (DEPTH, D_MODEL), jnp.float32)
    w_in = nrm(ks[2], (DEPTH, D_MODEL, N_IN), D_MODEL)
    forget_bias = jnp.linspace(1.0, 6.0, H_FOX, dtype=jnp.float32)[None, :] + 0.1 * jax.random.normal(ks[3], (DEPTH, H_FOX), jnp.float32)
    rel_bias = 0.5 * jax.random.normal(ks[4], (DEPTH, H_CHK, N_REL), jnp.float32)
    w_branch_a = nrm(ks[5], (DEPTH, W_FOX, D_MODEL), W_FOX)
    w_branch_b = nrm(ks[6], (DEPTH, W_CHK, D_MODEL), W_CHK)
    w_out = nrm(ks[7], (DEPTH, D_MODEL, D_MODEL), D_MODEL)
    norm2 = 1.0 + 0.02 * jax.random.normal(ks[8], (DEPTH, D_MODEL), jnp.float32)
    w_up = nrm(ks[9], (DEPTH, D_MODEL, D_FF), D_MODEL)
    w_down = nrm(ks[10], (DEPTH, D_FF, D_MODEL), D_FF)
    final_norm = 1.0 + 0.02 * jax.random.normal(ks[11], (D_MODEL,), jnp.float32)
    return {"x": x, "norm1": norm1, "w_in": w_in, "forget_bias": forget_bias,
            "rel_bias": rel_bias, "w_branch_a": w_branch_a, "w_branch_b": w_branch_b,
            "w_out": w_out, "norm2": norm2, "w_up": w_up, "w_down": w_down,
            "final_norm": final_norm}


def reference(x, norm1, w_in, forget_bias, rel_bias, w_branch_a, w_branch_b,
              w_out, norm2, w_up, w_down, final_norm):
    for l in range(DEPTH):
        h = rms_norm(x, norm1[l])
        proj = h @ w_in[l]
        qa, ka, va, fa, qc, kc, vc, gates = jnp.split(proj, SPLITS, axis=-1)
        log_f = jax.nn.log_sigmoid((fa + forget_bias[l]).astype(jnp.float32))
        log_f = log_f.transpose(0, 2, 1)
        o_a = from_heads(forgetting_attention(to_heads(qa, H_FOX), to_heads(ka, H_FOX),
                                              to_heads(va, H_FOX), log_f))
        o_b = from_heads(chunk_band_attention(to_heads(qc, H_CHK), to_heads(kc, H_CHK),
                                              to_heads(vc, H_CHK), rel_bias[l]))
        g_a, g_b = jnp.split(jax.nn.sigmoid(gates), N_BRANCH, axis=-1)
        merged = g_a * (o_a @ w_branch_a[l]) + g_b * (o_b @ w_branch_b[l])
        x = x + merged @ w_out[l]
        h2 = rms_norm(x, norm2[l])
        x = x + jnp.square(jax.nn.relu(h2 @ w_up[l])) @ w_down[l]
    return rms_norm(x, final_norm)
```

```python
import numpy as np
from contextlib import ExitStack
import concourse.bass as bass
import concourse.mybir as mybir
from concourse.bass_utils import run_bass_kernel_spmd

F32 = mybir.dt.float32
BF16 = mybir.dt.bfloat16
ALU = mybir.AluOpType
AF = mybir.ActivationFunctionType

D = 1024
L = 2
DFF = 4096
NIN = 5128
H = 8
EPS = 1e-6
NEGM = -30000.0
NSLOT = 24


class Buf:
    __slots__ = ("w", "r", "pw", "pr")

    def __init__(self):
        self.w = {}
        self.r = {}
        self.pw = {}
        self.pr = {}


class Eng:
    def __init__(self, name, eng, sem):
        self.name, self.eng, self.sem = name, eng, sem
        self.count = 0
        self.seen = {}
        self.slots = []
        self.slot_i = 0


class Ctx:
    def __init__(self, nc):
        self.nc = nc
        self.engs = {}
        for name, eng in (("pe", nc.tensor), ("act", nc.scalar), ("dve", nc.vector),
                          ("pool", nc.gpsimd), ("sp", nc.sync)):
            self.engs[name] = Eng(name, eng, nc.alloc_semaphore("sem_" + name))
        for q in ("sp", "pool", "act"):
            self.engs[q].slots = [[nc.alloc_semaphore(f"dq_{q}_{i}"), 0] for i in range(NSLOT)]
        self.fence_sem = nc.alloc_semaphore("fence")
        self.fence_val = 0
        self.fence_d = nc.dram_tensor("fence_d", [2, 64], F32, kind="Internal").ap()
        self.fence_src = None

    @staticmethod
    def _merge(d, sem, val):
        k = sem.num
        if k not in d or d[k][1] < val:
            d[k] = (sem, val)

    def _wait(self, e, deps):
        for k, (sem, val) in deps.items():
            if e.seen.get(k, 0) >= val:
                continue
            e.eng.wait_ge(sem, val)
            e.seen[k] = val

    def _deps(self, e, reads, writes, pwrites, is_dma):
        raw, oth = {}, {}
        for b in reads:
            for sem, val in b.w.values():
                self._merge(raw, sem, val)
        for b in writes:
            for sem, val in b.w.values():
                self._merge(oth, sem, val)
            for sem, val in b.r.values():
                self._merge(oth, sem, val)
        for b in pwrites:
            for d_ in (b.r, b.pr, b.pw):
                for sem, val in d_.values():
                    self._merge(oth, sem, val)
        own = e.sem.num
        deps = {}
        for k, (sem, val) in raw.items():
            if k == own and not is_dma and e.name == "pe":
                continue
            self._merge(deps, sem, val)
        for k, (sem, val) in oth.items():
            if k == own and not is_dma:
                continue
            self._merge(deps, sem, val)
        return deps

    def _record(self, tag, reads, writes, pwrites):
        sem, val = tag
        for b in reads:
            self._merge(b.r, sem, val)
        for b in writes:
            b.pw = b.w
            b.pr = b.r
            b.w = {sem.num: (sem, val)}
            b.r = {}
        for b in pwrites:
            self._merge(b.w, sem, val)

    def op(self, en, fn, reads=(), writes=(), pwrites=()):
        e = self.engs[en]
        self._wait(e, self._deps(e, reads, writes, pwrites, False))
        ins = fn(e.eng)
        e.count += 1
        ins.then_inc(e.sem, 1)
        self._record((e.sem, e.count), reads, writes, pwrites)

    def dma(self, q, out, in_, reads=(), writes=(), pwrites=()):
        e = self.engs[q]
        slot = e.slots[e.slot_i % NSLOT]
        e.slot_i += 1
        deps = self._deps(e, reads, writes, pwrites, True)
        if slot[1] > 0:
            self._merge(deps, slot[0], slot[1])
        self._wait(e, deps)
        e.eng.dma_start(out=out, in_=in_).then_inc(slot[0], 16)
        slot[1] += 16
        self._record((slot[0], slot[1]), reads, writes, pwrites)

    def fence(self):
        sp = self.engs["sp"]
        allv = {}
        for e in self.engs.values():
            if e.count > 0:
                self._merge(allv, e.sem, e.count)
            for s in e.slots:
                if s[1] > 0:
                    self._merge(allv, s[0], s[1])
        self._wait(sp, allv)
        sp.eng.dma_start(out=self.fence_d[1:2, 0:8], in_=self.fence_src).then_inc(self.fence_sem, 16)
        self.fence_val += 16
        for e in self.engs.values():
            e.eng.wait_ge(self.fence_sem, self.fence_val)
            for k, (sem, val) in allv.items():
                e.seen[k] = max(e.seen.get(k, 0), val)


def build(S, NTOK, R, last_only=None, dbg=False):
    assert R == 1 and NTOK == S
    HG = H // R
    NT = NTOK // 512
    NKB = S // 128
    nc = bass.Bass("TRN2", target_bir_lowering=False)
    cx = Ctx(nc)
    _uid = [0]

    def nm(n):
        _uid[0] += 1
        return f"{n}_{_uid[0]}"
    op, dma = cx.op, cx.dma

    def din(name, shape, dt=F32):
        return nc.dram_tensor(name, shape, dt, kind="ExternalInput").ap()

    skind = "ExternalOutput" if dbg else "Internal"

    def dscr(name, shape, dt):
        return nc.dram_tensor(name, shape, dt, kind=skind).ap()

    xT = din("xT", [D, NTOK])
    w_in = din("w_in", [L, D, NIN])
    w_a = din("w_a", [L, 512, D])
    w_b = din("w_b", [L, 512, D])
    w_out = din("w_out", [L, D, D])
    w_up = din("w_up", [L, D, DFF])
    w_down = din("w_down", [L, DFF, D])
    n1 = din("n1", [L, 128, 8])
    n2 = din("n2", [L, 128, 8])
    nf = din("nf", [128, 8])
    fb = din("fb", [L, 8, 1])
    tbias = din("tbias", [L, 128, HG * 3 * 128])
    outT = nc.dram_tensor("outT", [D, NTOK], F32, kind="ExternalOutput").ap()
    cx.fence_src = nf[0:1, :]

    XS = dscr("XS", [D, NTOK], F32)
    QA = dscr("QA", [HG * 64, S], BF16)
    QC = dscr("QC", [HG * 64, S], BF16)
    KA = dscr("KA", [HG * 64, S], BF16)
    KC = dscr("KC", [HG * 64, S], BF16)
    VA = dscr("VA", [128, HG, NKB, 64], BF16)
    VC = dscr("VC", [128, HG, NKB, 64], BF16)
    LF = dscr("LF", [HG, S], F32)
    CP = dscr("CP", [HG, 3, S], BF16)
    CN = dscr("CN", [HG, 3, S], BF16)
    G = dscr("G", [2 * D, NTOK], BF16)
    OA = dscr("OA", [512, NTOK], BF16)
    OC = dscr("OC", [512, NTOK], BF16)
    WinB = dscr("WinB", [D, NIN], BF16)
    WupB = dscr("WupB", [D, DFF], BF16)
    WdnB = dscr("WdnB", [DFF, D], BF16)
    WaB = dscr("WaB", [512, D], BF16)
    WbB = dscr("WbB", [512, D], BF16)
    WoB = dscr("WoB", [D, D], BF16)

    def fm(ap):
        return ap.rearrange("(c p) n -> p c n", p=128)

    PSALL = nc.alloc_psum_tensor("psall", [128, 8, 512], F32)
    PS = [PSALL[:, i, :] for i in range(8)]
    PSB = [Buf() for _ in range(8)]
    ones_bf = nc.alloc_sbuf_tensor("ones_bf", [128, 128], BF16)
    tri = nc.alloc_sbuf_tensor("tri", [128, 128], BF16)
    trif = nc.alloc_sbuf_tensor("trif", [128, 128], F32)
    nf_sb = nc.alloc_sbuf_tensor("nf_sb", [128, 8], F32)
    b_const = Buf()
    op("dve", lambda e: e.memset(ones_bf[:], 1.0), writes=[b_const])
    op("pool", lambda e: e.memset(trif[:], 1.0), writes=[b_const])
    op("pool", lambda e: e.affine_select(out=trif[:], in_=trif[:], pattern=[[1, 128]],
                                          compare_op=ALU.is_ge, fill=0.0, base=0,
                                          channel_multiplier=-1), reads=[b_const], writes=[b_const])
    op("dve", lambda e: e.tensor_copy(tri[:], trif[:]), reads=[b_const], writes=[b_const])
    wz = nc.alloc_sbuf_tensor("wz", [128, 512], BF16)
    op("pool", lambda e: e.memset(wz[:], 0.0), writes=[b_const])

    def warm_pe(n, bank, reads=()):
        for i in range(n):
            op("pe", lambda e: e.matmul(PS[bank][:], ident[:], wz[:], start=True, stop=True),
               reads=list(reads), writes=[PSB[bank]] if i == 0 else [], pwrites=[] if i == 0 else [PSB[bank]])
    ident = nc.alloc_sbuf_tensor("ident", [128, 128], BF16)
    identf = nc.alloc_sbuf_tensor("identf", [128, 128], F32)
    op("pool", lambda e: e.memset(identf[:], 1.0), writes=[b_const])
    op("pool", lambda e: e.affine_select(out=identf[:], in_=identf[:], pattern=[[1, 128]],
                                          compare_op=ALU.is_equal, fill=0.0, base=0,
                                          channel_multiplier=-1), reads=[b_const], writes=[b_const])
    op("dve", lambda e: e.tensor_copy(ident[:], identf[:]), reads=[b_const], writes=[b_const])
    dma("sp", nf_sb[:], nf, writes=[b_const])
    cx.fence()

    def load_weights(dst, src2d, ncols, kchunks, scale_ap, stg, stg_bufs, colstep):
        i = 0
        engs = ("dve", "act", "pool")
        for kc in range(kchunks):
            for c0 in range(0, ncols, colstep):
                c1 = min(ncols, c0 + colstep)
                sb = stg[i % len(stg)]
                bb = stg_bufs[i % len(stg)]
                dma("sp", sb[:, 0:c1 - c0], src2d[kc * 128:(kc + 1) * 128, c0:c1], writes=[bb])
                en = engs[i % 3]
                o_ap = dst[:, kc, c0:c1]
                i_ap = sb[:, 0:c1 - c0]
                if scale_ap is None:
                    if en == "act":
                        op(en, lambda e, o=o_ap, a=i_ap: e.copy(o, a), reads=[bb])
                    else:
                        op(en, (lambda e, o=o_ap, a=i_ap: e.tensor_copy(o, a)) if en == 'dve' else (lambda e, o=o_ap, a=i_ap: e.tensor_scalar(o, a, 1.0, None, ALU.mult)), reads=[bb])
                else:
                    sc = scale_ap[:, kc:kc + 1]
                    if en == "act":
                        op(en, lambda e, o=o_ap, a=i_ap, s=sc: e.activation(o, a, AF.Copy, scale=s), reads=[bb])
                    else:
                        op(en, lambda e, o=o_ap, a=i_ap, s=sc: e.tensor_scalar(o, a, s, None, ALU.mult), reads=[bb])
                i += 1

    def rmsnorm_h(xt_t, xb, sq, sqb, rstd, lnv, nb, hbf, hb, ntok, psi, do_sq=True, sq_eng="pool"):
        if do_sq:
            op(sq_eng, lambda e: e.tensor_tensor(sq[:], xt_t, xt_t, ALU.mult), reads=[xb], writes=[sqb])
        for c in range(8):
            op("pe", lambda e, c=c: e.matmul(PS[psi][:, 0:ntok], ones_bf[:], sq[:, c, :],
                                             start=(c == 0), stop=(c == 7)),
               reads=[sqb], writes=[PSB[psi]] if c == 0 else [], pwrites=[] if c == 0 else [PSB[psi]])
        op("act", lambda e: e.activation(lnv[:, 0:ntok], PS[psi][:, 0:ntok], AF.Ln, bias=EPS, scale=1.0 / D),
           reads=[PSB[psi]], writes=[nb])
        op("act", lambda e: e.activation(rstd[:, 0:ntok], lnv[:, 0:ntok], AF.Exp, scale=-0.5),
           reads=[nb], writes=[nb])
        if hbf is not None:
            for c in range(8):
                en = "dve" if c % 2 == 0 else "pool"
                op(en, lambda e, c=c: e.tensor_tensor(hbf[:, c, :], xt_t[:, c, :], rstd[:, 0:ntok], ALU.mult),
                   reads=[xb, nb], writes=[hb] if c == 0 else [], pwrites=[] if c == 0 else [hb])

    for l in range(L):
        xsrc = xT if l == 0 else XS
        with ExitStack() as es:
            W = es.enter_context(nc.sbuf_tensor(nm("W"), [128, 8, NIN], BF16))
            stgA = [es.enter_context(nc.sbuf_tensor(nm("stg"), [128, 641], F32)) for _ in range(4)]
            g1 = es.enter_context(nc.sbuf_tensor(nm("g1"), [128, 8], F32))
            fbs = es.enter_context(nc.sbuf_tensor(nm("fbs"), [8, 1], F32))
            xt0 = es.enter_context(nc.sbuf_tensor(nm("xt0"), [128, 8, 512], F32))
            xt1 = es.enter_context(nc.sbuf_tensor(nm("xt1"), [128, 8, 512], F32))
            sq0 = es.enter_context(nc.sbuf_tensor(nm("sq0"), [128, 8, 512], BF16))
            sq1 = es.enter_context(nc.sbuf_tensor(nm("sq1"), [128, 8, 512], BF16))
            hbf0 = es.enter_context(nc.sbuf_tensor(nm("hbf0"), [128, 8, 512], BF16))
            hbf1 = es.enter_context(nc.sbuf_tensor(nm("hbf1"), [128, 8, 512], BF16))
            rstd2 = es.enter_context(nc.sbuf_tensor(nm("rstd"), [128, 2, 512], F32))
            lnv2 = es.enter_context(nc.sbuf_tensor(nm("lnv"), [128, 2, 512], F32))
            ev0 = es.enter_context(nc.sbuf_tensor(nm("ev0"), [128, 4, 512], BF16))
            ev1 = es.enter_context(nc.sbuf_tensor(nm("ev1"), [128, 4, 512], BF16))
            ev2 = es.enter_context(nc.sbuf_tensor(nm("ev2"), [128, 4, 512], BF16))
            fz = es.enter_context(nc.sbuf_tensor(nm("fz"), [8, 5, 512], F32))
            xts, xbs = [xt0, xt1], [Buf(), Buf()]
            sqs, sqbs = [sq0, sq1], [Buf(), Buf()]
            hbfs, hbs = [hbf0, hbf1], [Buf(), Buf()]
            nbs = [Buf(), Buf()]
            evs, evb = [ev0, ev1, ev2], [Buf(), Buf(), Buf()]
            fzb, wb_ = Buf(), Buf()
            dma("sp", g1[:], n1[l], writes=[wb_])
            dma("sp", fbs[:], fb[l], writes=[wb_])
            cx.fence()
            if l == 0:
                load_weights(W, w_in[l], NIN, 8, g1, stgA, [Buf() for _ in range(4)], 641)
            else:
                for kc in range(8):
                    dma("sp" if kc % 2 == 0 else "act", W[:, kc, :], WinB[kc * 128:(kc + 1) * 128, :], pwrites=[wb_])
            cx.fence()
            pi = 0
            gi = 0
            groups = [("q", 0, QA), ("q", 512, QC), ("k", 1024, KA), ("k", 1536, KC),
                      ("g", 3072, 0), ("g", 3584, 512), ("g", 4096, 1024), ("g", 4608, 1536),
                      ("v", 2048, VA), ("v", 2560, VC), ("f", 5120, None)]

            def prologue_a(t):
                k = t % 2
                if t + 1 < NT:
                    dma("sp", xts[(t + 1) % 2][:], fm(xsrc)[:, :, (t + 1) * 512:(t + 2) * 512],
                        writes=[xbs[(t + 1) % 2]])
                rmsnorm_h(xts[k][:], xbs[k], sqs[k], sqbs[k], rstd2[:, k, :], lnv2[:, k, :], nbs[k], hbfs[k], hbs[k], 512, 0,
                          do_sq=(t == 0))

            dma("sp", xts[0][:], fm(xsrc)[:, :, 0:512], writes=[xbs[0]])
            prologue_a(0)
            for t in range(NT):
                hbf, hb = hbfs[t % 2], hbs[t % 2]
                tsl = slice(t * 512, (t + 1) * 512)
                for gidx, (kind, c0, dst) in enumerate(groups):
                    if gidx == 0 and t + 1 < NT:
                        k1 = (t + 1) % 2
                        op("dve", lambda e: e.tensor_tensor(sqs[k1][:], xts[k1][:], xts[k1][:], ALU.mult),
                           reads=[xbs[k1]], writes=[sqbs[k1]])
                    if gidx == 6 and t + 1 < NT:
                        prologue_a(t + 1)
                    if kind == "f":
                        for kc in range(8):
                            op("pe", lambda e, kc=kc: e.matmul(PS[7][0:8, :], W[:, kc, 5120:5128], hbf[:, kc, :],
                                                                start=(kc == 0), stop=(kc == 7)),
                               reads=[hb], writes=[PSB[7]] if kc == 0 else [], pwrites=[] if kc == 0 else [PSB[7]])
                        z, a_, e_, m_, lf_ = (fz[:, i, :] for i in range(5))
                        op("dve", lambda e: e.tensor_scalar(z, PS[7][0:8, :], fbs[:, 0:1], None, ALU.add), reads=[PSB[7]], writes=[fzb])
                        op("dve", lambda e: e.tensor_scalar(m_, z, -1.0, None, ALU.mult), reads=[fzb], writes=[fzb])
                        op("dve", lambda e: e.tensor_tensor(a_, z, m_, ALU.max), reads=[fzb], writes=[fzb])
                        op("act", lambda e: e.activation(e_, a_, AF.Exp, scale=-1.0), reads=[fzb], writes=[fzb])
                        op("act", lambda e: e.activation(e_, e_, AF.Ln, bias=1.0, scale=1.0), reads=[fzb], writes=[fzb])
                        op("dve", lambda e: e.tensor_scalar(m_, z, 0.0, None, ALU.min), reads=[fzb], writes=[fzb])
                        op("dve", lambda e: e.tensor_tensor(lf_, m_, e_, ALU.subtract), reads=[fzb], writes=[fzb])
                        dma("pool", LF[:, tsl], lf_, reads=[fzb])
                        continue
                    ev, eb = evs[gi % 3], evb[gi % 3]
                    gi += 1
                    for j in range(4):
                        p = 1 + pi % 6
                        pi += 1
                        for kc in range(8):
                            if kind == "v":
                                op("pe", lambda e, kc=kc: e.matmul(
                                    PS[p][:], hbf[:, kc, j * 128:(j + 1) * 128], W[:, kc, c0:c0 + 512],
                                    start=(kc == 0), stop=(kc == 7)),
                                   reads=[hb], writes=[PSB[p]] if kc == 0 else [], pwrites=[] if kc == 0 else [PSB[p]])
                            else:
                                cc = c0 + j * 128
                                op("pe", lambda e, kc=kc: e.matmul(
                                    PS[p][:], W[:, kc, cc:cc + 128], hbf[:, kc, :], start=(kc == 0), stop=(kc == 7)),
                                   reads=[hb], writes=[PSB[p]] if kc == 0 else [], pwrites=[] if kc == 0 else [PSB[p]])
                        wr = dict(writes=[eb]) if j == 0 else dict(pwrites=[eb])
                        if kind == "q":
                            op("dve", lambda e: e.tensor_scalar(ev[:, j, :], PS[p][:], 0.125, None, ALU.mult), reads=[PSB[p]], **wr)
                        elif kind == "k":
                            op("dve", lambda e: e.tensor_copy(ev[:, j, :], PS[p][:]), reads=[PSB[p]], **wr)
                        elif kind == "g":
                            op("act", lambda e: e.activation(ev[:, j, :], PS[p][:], AF.Sigmoid), reads=[PSB[p]], **wr)
                        else:
                            ev4 = ev[:].rearrange("p a b -> p (a b)").rearrange("p (h t d) -> p h t d", h=8, t=4)
                            src3 = PS[p][:].rearrange("p (h d) -> p h d", d=64)
                            if j % 2 == 0:
                                op("dve", lambda e: e.tensor_copy(ev4[:, :, j, :], src3), reads=[PSB[p]], **wr)
                            else:
                                op("act", lambda e: e.copy(ev4[:, :, j, :], src3), reads=[PSB[p]], **wr)
                    if kind == "g":
                        dma("pool", fm(G)[:, dst // 128:dst // 128 + 4, tsl], ev[:], reads=[eb])
                    elif kind == "v":
                        ev4 = ev[:].rearrange("p a b -> p (a b)").rearrange("p (h t d) -> p h t d", h=8, t=4)
                        dma("pool", dst[:, :, t * 4:(t + 1) * 4, :], ev4, reads=[eb])
                    else:
                        dma("pool", fm(dst)[:, :, tsl], ev[:], reads=[eb])
            cx.fence()
        if last_only == "A":
            break
        with ExitStack() as es:
            T0 = es.enter_context(nc.sbuf_tensor(nm("T0"), [8, S], F32))
            T1 = es.enter_context(nc.sbuf_tensor(nm("T1"), [8, S], F32))
            P3 = es.enter_context(nc.sbuf_tensor(nm("P3"), [8, 3, S], BF16))
            tb_ = Buf()
            dma("sp", T0[:], LF, writes=[tb_])
            op("dve", lambda e: e.tensor_tensor_scan(T1[:], T0[:], T0[:], 0.0, ALU.add, ALU.min), reads=[tb_], writes=[tb_])
            op("dve", lambda e: e.tensor_copy(P3[:, 0, :], T1[:]), reads=[tb_], writes=[tb_])
            op("dve", lambda e: e.tensor_tensor(T0[:], T1[:], P3[:, 0, :], ALU.subtract), reads=[tb_], writes=[tb_])
            op("dve", lambda e: e.tensor_copy(P3[:, 1, :], T0[:]), reads=[tb_], writes=[tb_])
            op("dve", lambda e: e.tensor_tensor(T1[:], T0[:], P3[:, 1, :], ALU.subtract), reads=[tb_], writes=[tb_])
            op("dve", lambda e: e.tensor_copy(P3[:, 2, :], T1[:]), reads=[tb_], writes=[tb_])
            dma("sp", CP, P3[:], reads=[tb_])
            cx.fence()
            op("dve", lambda e: e.tensor_scalar(P3[:], P3[:], -1.0, None, ALU.mult), writes=[tb_])
            dma("sp", CN, P3[:], reads=[tb_])
            cx.fence()
        with ExitStack() as es:
            KT0 = es.enter_context(nc.sbuf_tensor(nm("KT0"), [70, S], BF16))
            KT1 = es.enter_context(nc.sbuf_tensor(nm("KT1"), [70, S], BF16))
            QT0 = es.enter_context(nc.sbuf_tensor(nm("QT0"), [70, S], BF16))
            QT1 = es.enter_context(nc.sbuf_tensor(nm("QT1"), [70, S], BF16))
            VT0 = es.enter_context(nc.sbuf_tensor(nm("VT0"), [128, NKB, 128], BF16))
            VT1 = es.enter_context(nc.sbuf_tensor(nm("VT1"), [128, NKB, 128], BF16))
            PT = es.enter_context(nc.sbuf_tensor(nm("PT"), [128, 6, 512], BF16))
            BSh = es.enter_context(nc.sbuf_tensor(nm("BSh"), [128, HG, 5, 128], BF16))
            BSl = es.enter_context(nc.sbuf_tensor(nm("BSl"), [128, HG, 5, 128], BF16))
            Eo = es.enter_context(nc.sbuf_tensor(nm("Eo"), [128, 2, 512], F32))
            Dn = es.enter_context(nc.sbuf_tensor(nm("Dn"), [64, 2, 512], F32))
            On = es.enter_context(nc.sbuf_tensor(nm("On"), [64, 2, 512], BF16))
            BS = es.enter_context(nc.sbuf_tensor(nm("BS"), [128, HG, 5, 128], F32))
            KT, QT, VT = [KT0, KT1], [QT0, QT1], [VT0, VT1]
            kvb = [Buf(), Buf()]
            ptb = [Buf() for _ in range(3)]
            ptc = [Buf() for _ in range(6)]
            sfb = [Buf() for _ in range(4)]
            eob, dnb, onb = [Buf(), Buf()], [Buf(), Buf()], [Buf(), Buf()]
            cb = Buf()
            tbv = tbias[l].rearrange("p (h k t) -> p h k t", h=HG, k=3)
            dma("sp", BS[:, :, 0:3, :], tbv, writes=[cb])
            dma("sp", BS[:, :, 3, :], tbv[:, :, 2, :], pwrites=[cb])
            dma("sp", BS[:, :, 4, :], tbv[:, :, 2, :], pwrites=[cb])
            for b in range(2):
                op("dve", lambda e, b=b: e.memset(KT[b][64:70, :], 1.0), writes=[cb])
                op("pool", lambda e, b=b: e.memset(QT[b][64:70, :], 1.0), writes=[cb])
                op("pool", lambda e, b=b: e.memset(VT[b][:, :, 64:128], 1.0), writes=[cb])
            cx.fence()
            op("pool", lambda e: e.memset(BS[64:128, :, 0, 0:64], NEGM), writes=[cb])
            op("pool", lambda e: e.memset(BS[0:64, :, 4, 64:128], NEGM), writes=[cb])
            cx.fence()
            fl = "p h k t -> p (h k t)"
            op("dve", lambda e: e.tensor_copy(BSh[:].rearrange(fl), BS[:].rearrange(fl)), writes=[cb])
            cx.fence()
            op("dve", lambda e: e.tensor_tensor(BS[:].rearrange(fl), BS[:].rearrange(fl), BSh[:].rearrange(fl), ALU.subtract), writes=[cb])
            cx.fence()
            op("dve", lambda e: e.tensor_copy(BSl[:].rearrange(fl), BS[:].rearrange(fl)), writes=[cb])
            cx.fence()
            oi = 0

            def load_head(b, Ksrc, Qsrc, Vsrc, hl, fox):
                rs = slice(hl * 64, (hl + 1) * 64)
                dma("sp", KT[b][0:64, :], Ksrc[rs, :], pwrites=[kvb[b]])
                dma("sp", QT[b][0:64, :], Qsrc[rs, :], pwrites=[kvb[b]])
                if fox:
                    dma("sp", KT[b][64:67, :], CN[hl], pwrites=[kvb[b]])
                    dma("sp", QT[b][67:70, :], CP[hl], pwrites=[kvb[b]])
                vsrc = Vsrc[:, hl, :, :]
                half = NKB // 2
                dma("sp", VT[b][:, 0:half, 0:64], vsrc[:, 0:half, :], pwrites=[kvb[b]])
                dma("sp", VT[b][:, half:NKB, 0:64], vsrc[:, half:NKB, :], pwrites=[kvb[b]])

            pending = []
            pcs = [es.enter_context(nc.sbuf_tensor(nm("pcs"), [128, 2048], F32)) for _ in range(2)]
            pcd = [es.enter_context(nc.sbuf_tensor(nm("pcd"), [128, 2048], BF16)) for _ in range(2)]
            gp1 = es.enter_context(nc.sbuf_tensor(nm("gp1"), [128, 8], F32))
            gp2 = es.enter_context(nc.sbuf_tensor(nm("gp2"), [128, 8], F32))
            pcsb, pcdb, gpb = [Buf(), Buf()], [Buf(), Buf()], Buf()
            dma("sp", gp2[:], n2[l], writes=[gpb])
            if l + 1 < L:
                dma("sp", gp1[:], n1[l + 1], pwrites=[gpb])
            pc_steps = []
            jobs = [(w_a[l], WaB, D, 4, None), (w_b[l], WbB, D, 4, None), (w_out[l], WoB, D, 8, None),
                    (w_up[l], WupB, DFF, 8, gp2), (w_down[l], WdnB, D, 32, None)]
            if l + 1 < L:
                jobs.append((w_in[l + 1], WinB, NIN, 8, gp1))
            for src2d, dst2d, ncols, kch, sc in jobs:
                for kc in range(kch):
                    for c0 in range(0, ncols, 2048):
                        pc_steps.append((src2d, dst2d, kc, c0, min(ncols, c0 + 2048), sc))
            pc_i = [0]
            pc_prev = []

            def precast_step():
                if pc_i[0] >= len(pc_steps):
                    return
                i = pc_i[0]
                pc_i[0] += 1
                src2d, dst2d, kc, c0, c1, sc = pc_steps[i]
                k = i % 2
                w = c1 - c0
                rows = slice(kc * 128, (kc + 1) * 128)
                if pc_prev:
                    pd, pk, pw, prow, pc0, pc1 = pc_prev.pop()
                    dma("sp", pd[prow, pc0:pc1], pcd[pk][:, 0:pw], reads=[pcdb[pk]])
                dma("sp", pcs[k][:, 0:w], src2d[rows, c0:c1], writes=[pcsb[k]])
                if sc is None:
                    op("dve", lambda e: e.tensor_copy(pcd[k][:, 0:w], pcs[k][:, 0:w]),
                       reads=[pcsb[k]], writes=[pcdb[k]])
                else:
                    op("dve", lambda e: e.tensor_scalar(pcd[k][:, 0:w], pcs[k][:, 0:w], sc[:, kc:kc + 1], None, ALU.mult),
                       reads=[pcsb[k], gpb], writes=[pcdb[k]])
                pc_prev.append((dst2d, k, w, rows, c0, c1))

            def epilogue(o, psb, dst, hl, T, copy_eng="dve"):
                precast_step()
                if copy_eng == "dve":
                    op("dve", lambda e: e.tensor_copy(Eo[:, o, :], PS[psb][:]), reads=[PSB[psb]], writes=[eob[o]])
                else:
                    op("act", lambda e: e.copy(Eo[:, o, :], PS[psb][:]), reads=[PSB[psb]], writes=[eob[o]])
                dma("sp", Dn[:, o, :], Eo[64:128, o, :], reads=[eob[o]], writes=[dnb[o]])

                def part2():
                    if dst is OC:
                        op("act", lambda e: e.activation(Dn[:, o, :], Dn[:, o, :], AF.Ln), reads=[dnb[o]], writes=[dnb[o]])
                        op("act", lambda e: e.activation(Dn[:, o, :], Dn[:, o, :], AF.Exp, scale=-1.0), reads=[dnb[o]], writes=[dnb[o]])
                    else:
                        op("dve", lambda e: e.reciprocal(Dn[:, o, :], Dn[:, o, :]), reads=[dnb[o]], writes=[dnb[o]])
                    op("dve", lambda e: e.tensor_tensor(On[:, o, :], Eo[0:64, o, :], Dn[:, o, :], ALU.mult),
                       reads=[eob[o], dnb[o]], writes=[onb[o]])
                    dma("sp", dst[hl * 64:(hl + 1) * 64, T * 512:(T + 1) * 512], On[:, o, :], reads=[onb[o]])
                pending.append([4, part2])

            def tick(flush=False):
                for it in list(pending):
                    it[0] -= 1
                    if flush or it[0] <= 0:
                        it[1]()
                        pending.remove(it)

            PAIRB = [0, 2, 6]
            heads = [("fox", hl) for hl in range(HG)] + [("chk", hl) for hl in range(HG)]
            def load_head_idx(hi):
                kind_, hl_ = heads[hi]
                if kind_ == "fox":
                    load_head(hi % 2, KA, QA, VA, hl_, True)
                else:
                    load_head(hi % 2, KC, QC, VC, hl_, False)

            load_head_idx(0)
            for hidx, (kind, hl) in enumerate(heads):
                b = hidx % 2
                fox = kind == "fox"
                if hidx + 1 < len(heads):
                    load_head_idx(hidx + 1)
                if fox:
                    units = []
                    t_order = []
                    lo_, hi_t = 0, NT - 1
                    while lo_ <= hi_t:
                        t_order.append(hi_t)
                        if lo_ != hi_t:
                            t_order.append(lo_)
                        lo_ += 1
                        hi_t -= 1
                    for T in t_order:
                        o = oi % 2
                        oi += 1
                        tl = []
                        for kb in range(0, 4 * T, 2):
                            tl.append(dict(T=T, o=o, diag=False, blocks=[(kb, 0), (kb + 1, 0)]))
                        tl.append(dict(T=T, o=o, diag=True, blocks=[(4 * T, 0), (4 * T + 1, 128)]))
                        tl.append(dict(T=T, o=o, diag=True, blocks=[(4 * T + 2, 256), (4 * T + 3, 384)]))
                        tl[0]["first"] = True
                        tl[-1]["last"] = True
                        units += tl

                    def f_qk(ui):
                        u = units[ui]
                        pb = PAIRB[ui % 3]
                        T = u["T"]
                        for jj, (kb, q0) in enumerate(u["blocks"]):
                            op("pe", lambda e: e.matmul(PS[pb + jj][:, q0:512], KT[b][0:70, kb * 128:(kb + 1) * 128],
                                                        QT[b][0:70, T * 512 + q0:(T + 1) * 512], start=True, stop=True),
                               reads=[kvb[b]], writes=[PSB[pb + jj]])

                    def f_exp(ui):
                        u = units[ui]
                        pb = PAIRB[ui % 3]
                        pi_ = ui % 3
                        if not u["diag"]:
                            op("act", lambda e: e.activation(PT[:, 2 * pi_:2 * pi_ + 2, :], PSALL[:, pb:pb + 2, :], AF.Exp),
                               reads=[PSB[pb], PSB[pb + 1]], writes=[ptb[pi_]])
                        else:
                            for jj, (kb, q0) in enumerate(u["blocks"]):
                                wr = dict(writes=[ptb[pi_]]) if jj == 0 else dict(pwrites=[ptb[pi_]])
                                op("act", lambda e: e.activation(PT[:, 2 * pi_ + jj, q0:512], PS[pb + jj][:, q0:512], AF.Exp),
                                   reads=[PSB[pb + jj]], **wr)
                            for jj, (kb, q0) in enumerate(u["blocks"]):
                                op("pool", lambda e: e.tensor_tensor(PT[:, 2 * pi_ + jj, q0:q0 + 128], PT[:, 2 * pi_ + jj, q0:q0 + 128],
                                                                     tri[:], ALU.mult),
                                   reads=[ptb[pi_]], writes=[ptb[pi_]])

                    def f_pv(ui):
                        u = units[ui]
                        pi_ = ui % 3
                        psb = 4 + u["o"]
                        for jj, (kb, q0) in enumerate(u["blocks"]):
                            st = bool(u.get("first")) and jj == 0
                            sp_ = bool(u.get("last")) and jj == 1
                            op("pe", lambda e: e.matmul(PS[psb][:, q0:512], VT[b][:, kb, :], PT[:, 2 * pi_ + jj, q0:512],
                                                        start=st, stop=sp_),
                               reads=[ptb[pi_], kvb[b]], writes=[PSB[psb]] if st else [], pwrites=[] if st else [PSB[psb]])
                        if u.get("last"):
                            epilogue(u["o"], psb, OA, hl, u["T"])

                    LOOK = 2
                    for ui in range(min(LOOK, len(units))):
                        f_qk(ui)
                    for ui in range(len(units)):
                        if ui + LOOK < len(units):
                            f_qk(ui + LOOK)
                        f_exp(ui)
                        f_pv(ui)
                        tick()
                    tick(True)
                else:
                    if hl == 0:
                        cx.fence()
                    units = []
                    for T in range(NT):
                        o = oi % 2
                        oi += 1
                        i0 = 4 * T
                        tl = []
                        for j in [i0, i0 - 1, i0 - 2, i0 - 3, i0 - 4, i0 + 1, i0 + 2, i0 + 3]:
                            if j < 0:
                                continue
                            a = max(j, i0)
                            bq = min(j + 4, i0 + 3)
                            tl.append(dict(T=T, o=o, j=j, col0=(a - i0) * 128, n=(bq - a + 1) * 128, dk0=a - j, dk1=bq - j + 1))
                        tl[0]["first"] = True
                        tl[-1]["last"] = True
                        units += tl

                    SBK = [0, 1, 2, 3, 6, 7]

                    def c_qk_multi(uis):
                        for ui in uis:
                            u = units[ui]
                            s = SBK[ui % 6]
                            j, n = u["j"], u["n"]
                            qc0 = u["T"] * 512 + u["col0"]
                            op("pe", lambda e: e.matmul(PS[s][:, 0:n], KT[b][0:64, j * 128:(j + 1) * 128], QT[b][0:64, qc0:qc0 + n],
                                                        start=True, stop=False), reads=[kvb[b]], writes=[PSB[s]])
                        for ui in uis:
                            u = units[ui]
                            s = SBK[ui % 6]
                            n = u["n"]
                            bh = BSh[:, hl, u["dk0"]:u["dk1"], :].rearrange("p a t -> p (a t)")
                            bl = BSl[:, hl, u["dk0"]:u["dk1"], :].rearrange("p a t -> p (a t)")
                            op("pe", lambda e: e.matmul(PS[s][:, 0:n], ident[:], bh, start=False, stop=False), pwrites=[PSB[s]])
                            op("pe", lambda e: e.matmul(PS[s][:, 0:n], ident[:], bl, start=False, stop=True), pwrites=[PSB[s]])

                    def c_rest(ui):
                        u = units[ui]
                        s = SBK[ui % 6]
                        r = ui % 6
                        j, n = u["j"], u["n"]
                        psb = 4 + u["o"]
                        op("act", lambda e: e.activation(PT[:, r, 0:n], PS[s][:, 0:n], AF.Exp), reads=[PSB[s]], writes=[ptc[r]])
                        st = bool(u.get("first"))
                        op("pe", lambda e: e.matmul(PS[psb][:, u["col0"]:u["col0"] + n], VT[b][:, j, :], PT[:, r, 0:n],
                                                    start=st, stop=bool(u.get("last"))),
                           reads=[ptc[r], kvb[b]], writes=[PSB[psb]] if st else [], pwrites=[] if st else [PSB[psb]])
                        if u.get("last"):
                            epilogue(u["o"], psb, OC, hl, u["T"], "dve")

                    nq = [0]

                    def emit_pair():
                        lst = [x for x in (nq[0], nq[0] + 1) if x < len(units)]
                        c_qk_multi(lst)
                        nq[0] += 2

                    while nq[0] < min(4, len(units)):
                        emit_pair()
                    for ui in range(len(units)):
                        if nq[0] < len(units) and nq[0] - ui <= 4:
                            emit_pair()
                        c_rest(ui)
                        tick()
                    tick(True)
            while pc_i[0] < len(pc_steps):
                precast_step()
            if pc_prev:
                pd, pk, pw, prow, pc0, pc1 = pc_prev.pop()
                dma("sp", pd[prow, pc0:pc1], pcd[pk][:, 0:pw], reads=[pcdb[pk]])
            cx.fence()
        if last_only == "B":
            break
        with ExitStack() as es:
            Wa = es.enter_context(nc.sbuf_tensor(nm("Wa"), [128, 4, D], BF16))
            Wb = es.enter_context(nc.sbuf_tensor(nm("Wb"), [128, 4, D], BF16))
            Wo = es.enter_context(nc.sbuf_tensor(nm("Wo"), [128, 8, D], BF16))
            stgC = [es.enter_context(nc.sbuf_tensor(nm("stg"), [128, 512], F32)) for _ in range(4)]
            xt0 = es.enter_context(nc.sbuf_tensor(nm("xt0"), [128, 8, 512], F32))
            xt1 = es.enter_context(nc.sbuf_tensor(nm("xt1"), [128, 8, 512], F32))
            ot0 = es.enter_context(nc.sbuf_tensor(nm("ot0"), [128, 8, 512], BF16))
            ot1 = es.enter_context(nc.sbuf_tensor(nm("ot1"), [128, 8, 512], BF16))
            gt0 = es.enter_context(nc.sbuf_tensor(nm("gt0"), [128, 16, 512], BF16))
            gt1 = es.enter_context(nc.sbuf_tensor(nm("gt1"), [128, 16, 512], BF16))
            mt0 = es.enter_context(nc.sbuf_tensor(nm("mt0"), [128, 8, 512], BF16))
            mt1 = es.enter_context(nc.sbuf_tensor(nm("mt1"), [128, 8, 512], BF16))
            t1 = es.enter_context(nc.sbuf_tensor(nm("t1"), [128, 2, 512], F32))
            t2 = es.enter_context(nc.sbuf_tensor(nm("t2"), [128, 2, 512], F32))
            stgs, stb = stgC, [Buf() for _ in range(4)]
            wlb = Buf()
            dma("sp", Wa[:], fm(WaB), pwrites=[wlb])
            dma("act", Wb[:], fm(WbB), pwrites=[wlb])
            dma("sp", Wo[:, 0:4, :], fm(WoB)[:, 0:4, :], pwrites=[wlb])
            dma("act", Wo[:, 4:8, :], fm(WoB)[:, 4:8, :], pwrites=[wlb])
            cx.fence()
            xts, xbs = [xt0, xt1], [Buf(), Buf()]
            ots, obs = [ot0, ot1], [Buf(), Buf()]
            gts, gbs = [gt0, gt1], [Buf(), Buf()]
            mts, mbs = [mt0, mt1], [Buf(), Buf()]
            tbs = [Buf(), Buf()]
            pi = 0

            def loads(t):
                tsl = slice(t * 512, (t + 1) * 512)
                k = t % 2
                dma("sp", ots[k][:, 0:4, :], fm(OA)[:, :, tsl], pwrites=[obs[k]])
                dma("sp", ots[k][:, 4:8, :], fm(OC)[:, :, tsl], pwrites=[obs[k]])
                dma("sp", gts[k][:, 0:8, :], fm(G)[:, 0:8, tsl], pwrites=[gbs[k]])
                dma("sp", gts[k][:, 8:16, :], fm(G)[:, 8:16, tsl], pwrites=[gbs[k]])
                dma("act", xts[k][:], fm(xsrc)[:, :, tsl], writes=[xbs[k]])

            def merge_part(t):
                nonlocal pi
                k = t % 2
                ot, gt, mt, mb = ots[k], gts[k], mts[k], mbs[k]
                for oc in range(8):
                    pa = pi % 8
                    pb = (pi + 1) % 8
                    pi += 2
                    for kc in range(4):
                        op("pe", lambda e, kc=kc: e.matmul(PS[pa][:], Wa[:, kc, oc * 128:(oc + 1) * 128], ot[:, kc, :],
                                                           start=(kc == 0), stop=(kc == 3)),
                           reads=[obs[k]], writes=[PSB[pa]] if kc == 0 else [], pwrites=[] if kc == 0 else [PSB[pa]])
                    for kc in range(4):
                        op("pe", lambda e, kc=kc: e.matmul(PS[pb][:], Wb[:, kc, oc * 128:(oc + 1) * 128], ot[:, 4 + kc, :],
                                                           start=(kc == 0), stop=(kc == 3)),
                           reads=[obs[k]], writes=[PSB[pb]] if kc == 0 else [], pwrites=[] if kc == 0 else [PSB[pb]])
                    tk = oc % 2
                    op("dve", lambda e: e.tensor_tensor(t1[:, tk, :], PS[pa][:], gt[:, oc, :], ALU.mult),
                       reads=[PSB[pa], gbs[k]], writes=[tbs[tk]])
                    op("dve", lambda e: e.tensor_tensor(t2[:, tk, :], PS[pb][:], gt[:, 8 + oc, :], ALU.mult),
                       reads=[PSB[pb], gbs[k]], pwrites=[tbs[tk]])
                    op("pool", lambda e: e.tensor_tensor(mt[:, oc, :], t1[:, tk, :], t2[:, tk, :], ALU.add),
                       reads=[tbs[tk]], writes=[mb] if oc == 0 else [], pwrites=[] if oc == 0 else [mb])

            def out_part(t):
                nonlocal pi
                k = t % 2
                xt_t, mt, mb = xts[k], mts[k], mbs[k]
                for oc in range(8):
                    py = pi % 8
                    pi += 1
                    for kc in range(8):
                        op("pe", lambda e, kc=kc: e.matmul(PS[py][:], Wo[:, kc, oc * 128:(oc + 1) * 128], mt[:, kc, :],
                                                           start=(kc == 0), stop=(kc == 7)),
                           reads=[mb], writes=[PSB[py]] if kc == 0 else [], pwrites=[] if kc == 0 else [PSB[py]])
                    op("dve", lambda e: e.tensor_tensor(xt_t[:, oc, :], xt_t[:, oc, :], PS[py][:], ALU.add),
                       reads=[PSB[py]], pwrites=[xbs[k]])
                dma("pool", fm(XS)[:, :, t * 512:(t + 1) * 512], xt_t[:], reads=[xbs[k]])

            loads(0)
            merge_part(0)
            for t in range(NT):
                if t + 1 < NT:
                    loads(t + 1)
                    merge_part(t + 1)
                out_part(t)
            cx.fence()
        if last_only == "C1":
            break
        TT = 256
        NT2 = NTOK // TT
        with ExitStack() as es:
            Wu = es.enter_context(nc.sbuf_tensor(nm("Wu"), [128, 8, DFF], BF16))
            Wd = es.enter_context(nc.sbuf_tensor(nm("Wd"), [128, 32, D], BF16))
            g2 = es.enter_context(nc.sbuf_tensor(nm("g2"), [128, 8], F32))
            with ExitStack() as es:
                gb_ = Buf()
                dma("sp", g2[:], n2[l], writes=[gb_])
                cx.fence()
                for kc in range(8):
                    dma("sp" if kc % 2 == 0 else "act", Wu[:, kc, :], WupB[kc * 128:(kc + 1) * 128, :], pwrites=[gb_])
                for q4 in range(4):
                    dma("sp" if q4 % 2 == 0 else "act", Wd[:, q4 * 8:(q4 + 1) * 8, :], fm(WdnB)[:, q4 * 8:(q4 + 1) * 8, :], pwrites=[gb_])
                cx.fence()
            with ExitStack() as es:
                xt0 = es.enter_context(nc.sbuf_tensor(nm("xt0"), [128, 8, TT], F32))
                xt1 = es.enter_context(nc.sbuf_tensor(nm("xt1"), [128, 8, TT], F32))
                sq0 = es.enter_context(nc.sbuf_tensor(nm("sq0"), [128, 8, TT], BF16))
                sq1 = es.enter_context(nc.sbuf_tensor(nm("sq1"), [128, 8, TT], BF16))
                hbf0 = es.enter_context(nc.sbuf_tensor(nm("hbf0"), [128, 8, TT], BF16))
                hbf1 = es.enter_context(nc.sbuf_tensor(nm("hbf1"), [128, 8, TT], BF16))
                rstd2 = es.enter_context(nc.sbuf_tensor(nm("rstd"), [128, 2, TT], F32))
                lnv2 = es.enter_context(nc.sbuf_tensor(nm("lnv"), [128, 2, TT], F32))
                u = es.enter_context(nc.sbuf_tensor(nm("u"), [128, 32, TT], BF16))
                rr = es.enter_context(nc.sbuf_tensor(nm("rr"), [128, 2, TT], F32))
                xts, xbs = [xt0, xt1], [Buf(), Buf()]
                sqs, sqbs = [sq0, sq1], [Buf(), Buf()]
                hbfs, hbs = [hbf0, hbf1], [Buf(), Buf()]
                nbs = [Buf(), Buf()]
                ub = Buf()
                rrb = [Buf() for _ in range(2)]
                pi = 0
                ri = 0

                def prologue_c(t):
                    k = t % 2
                    rmsnorm_h(xts[k][:], xbs[k], sqs[k], sqbs[k], rstd2[:, k, :], lnv2[:, k, :], nbs[k], hbfs[k], hbs[k], TT, 0)

                dma("sp", xts[0][:], fm(XS)[:, :, 0:TT], writes=[xbs[0]])
                prologue_c(0)
                for t in range(NT2):
                    k = t % 2
                    xt_t, xb, hbf, hb = xts[k], xbs[k], hbfs[k], hbs[k]
                    if t + 1 < NT2:
                        dma("sp", xts[(t + 1) % 2][:], fm(XS)[:, :, (t + 1) * TT:(t + 2) * TT], writes=[xbs[(t + 1) % 2]])
                    for oc in range(32):
                        p = 1 + pi % 5
                        pi += 1
                        for kc in range(8):
                            op("pe", lambda e, kc=kc: e.matmul(PS[p][:, 0:TT], Wu[:, kc, oc * 128:(oc + 1) * 128], hbf[:, kc, :],
                                                               start=(kc == 0), stop=(kc == 7)),
                               reads=[hb], writes=[PSB[p]] if kc == 0 else [], pwrites=[] if kc == 0 else [PSB[p]])
                        r = ri % 2
                        ri += 1
                        op("act", lambda e: e.activation(rr[:, r, :], PS[p][:, 0:TT], AF.Relu), reads=[PSB[p]], writes=[rrb[r]])
                        op("dve", lambda e: e.tensor_tensor(u[:, oc, :], rr[:, r, :], rr[:, r, :], ALU.mult),
                           reads=[rrb[r]], writes=[ub] if oc == 0 else [], pwrites=[] if oc == 0 else [ub])
                    if t + 1 < NT2:
                        prologue_c(t + 1)
                    for oc in range(8):
                        p = 6 + pi % 2
                        pi += 1
                        for kc in range(32):
                            op("pe", lambda e, kc=kc: e.matmul(PS[p][:, 0:TT], Wd[:, kc, oc * 128:(oc + 1) * 128], u[:, kc, :],
                                                               start=(kc == 0), stop=(kc == 31)),
                               reads=[ub], writes=[PSB[p]] if kc == 0 else [], pwrites=[] if kc == 0 else [PSB[p]])
                        op("dve", lambda e: e.tensor_tensor(xt_t[:, oc, :], xt_t[:, oc, :], PS[p][:, 0:TT], ALU.add),
                           reads=[PSB[p]], pwrites=[xb])
                    if l == L - 1:
                        rmsnorm_h(xt_t[:], xb, sqs[k], sqbs[k], rstd2[:, k, :], lnv2[:, k, :], nbs[k], None, None, TT, 0)
                        for c in range(8):
                            op("dve", lambda e, c=c: e.scalar_tensor_tensor(xt_t[:, c, :], xt_t[:, c, :], nf_sb[:, c:c + 1], rstd2[:, k, :],
                                                                          ALU.mult, ALU.mult),
                               reads=[nbs[k]], pwrites=[xb])
                        dma("pool", fm(outT)[:, :, t * TT:(t + 1) * TT], xt_t[:], reads=[xb])
                    else:
                        dma("pool", fm(XS)[:, :, t * TT:(t + 1) * TT], xt_t[:], reads=[xb])
                cx.fence()
    cx.fence()
    return nc


def prep_weights(inp):
    w = np.asarray(inp["w_in"], np.float32)
    perm = np.concatenate([np.arange(0, 512), np.arange(1544, 2056), np.arange(512, 1024), np.arange(2056, 2568),
                           np.arange(1024, 1536), np.arange(2568, 3080), np.arange(3080, 5128), np.arange(1536, 1544)])
    w_in_p = np.ascontiguousarray(w[:, :, perm])

    def pc(v):
        v = np.asarray(v, np.float32)
        return np.ascontiguousarray(np.swapaxes(v.reshape(v.shape[:-1] + (8, 128)), -1, -2))
    rb = np.asarray(inp["rel_bias"], np.float32)
    sp = np.arange(128)[:, None]
    tp = np.arange(128)[None, :]
    tiles = []
    for k in range(3):
        idx = np.clip(128 * k + tp - sp, -128, 128) + 128
        tiles.append(rb[:, :, idx])
    tb = np.stack(tiles, axis=2)
    tb = np.ascontiguousarray(tb.transpose(0, 3, 1, 2, 4)).reshape(L, 128, H * 3 * 128)
    return dict(
        w_in=w_in_p,
        w_a=np.ascontiguousarray(inp["w_branch_a"], np.float32),
        w_b=np.ascontiguousarray(inp["w_branch_b"], np.float32),
        w_out=np.ascontiguousarray(inp["w_out"], np.float32),
        w_up=np.ascontiguousarray(inp["w_up"], np.float32),
        w_down=np.ascontiguousarray(inp["w_down"], np.float32),
        n1=pc(inp["norm1"]), n2=pc(inp["norm2"]), nf=pc(inp["final_norm"]),
        fb=np.ascontiguousarray(np.asarray(inp["forget_bias"], np.float32)[:, :, None]),
        tbias=tb,
    )


def kernel(**inp):
    x = np.asarray(inp["x"], np.float32)
    B, S, _ = x.shape
    wts = prep_weights(inp)
    zw = {k: np.zeros_like(v) for k, v in wts.items()}
    nc = build(S, S, 1)
    active = {2 * b: b for b in range(B)}
    in_maps = []
    for c in range(8):
        if c in active:
            m = dict(wts)
            m["xT"] = np.ascontiguousarray(x[active[c]].T)
        else:
            m = dict(zw)
            m["xT"] = np.zeros((D, S), np.float32)
        in_maps.append(m)
    res = run_bass_kernel_spmd(nc, in_maps, core_ids=list(range(8)))
    out = np.stack([np.ascontiguousarray(res.results[2 * b]["outT"].T) for b in range(B)], axis=0)
    return out.astype(np.float32)
```

```python
import numpy as np
from contextlib import ExitStack
import concourse.bass as bass
import concourse.mybir as mybir
from concourse.bass_utils import run_bass_kernel_spmd

F32 = mybir.dt.float32
BF16 = mybir.dt.bfloat16
ALU = mybir.AluOpType
AF = mybir.ActivationFunctionType

D = 1024
L = 2
DFF = 4096
NIN = 5128
H = 8
EPS = 1e-6
NEGM = -30000.0
NSLOT = 24


class Buf:
    __slots__ = ("w", "r", "pw", "pr")

    def __init__(self):
        self.w = {}
        self.r = {}
        self.pw = {}
        self.pr = {}


class Eng:
    def __init__(self, name, eng, sem):
        self.name, self.eng, self.sem = name, eng, sem
        self.count = 0
        self.seen = {}
        self.slots = []
        self.slot_i = 0


class Ctx:
    def __init__(self, nc):
        self.nc = nc
        self.engs = {}
        for name, eng in (("pe", nc.tensor), ("act", nc.scalar), ("dve", nc.vector),
                          ("pool", nc.gpsimd), ("sp", nc.sync)):
            self.engs[name] = Eng(name, eng, nc.alloc_semaphore("sem_" + name))
        for q in ("sp", "pool", "act"):
            self.engs[q].slots = [[nc.alloc_semaphore(f"dq_{q}_{i}"), 0] for i in range(NSLOT)]
        self.fence_sem = nc.alloc_semaphore("fence")
        self.fence_val = 0
        self.fence_d = nc.dram_tensor("fence_d", [2, 64], F32, kind="Internal").ap()
        self.fence_src = None

    @staticmethod
    def _merge(d, sem, val):
        k = sem.num
        if k not in d or d[k][1] < val:
            d[k] = (sem, val)

    def _wait(self, e, deps):
        for k, (sem, val) in deps.items():
            if e.seen.get(k, 0) >= val:
                continue
            e.eng.wait_ge(sem, val)
            e.seen[k] = val

    def _deps(self, e, reads, writes, pwrites, is_dma):
        raw, oth = {}, {}
        for b in reads:
            for sem, val in b.w.values():
                self._merge(raw, sem, val)
        for b in writes:
            for sem, val in b.w.values():
                self._merge(oth, sem, val)
            for sem, val in b.r.values():
                self._merge(oth, sem, val)
        for b in pwrites:
            for d_ in (b.r, b.pr, b.pw):
                for sem, val in d_.values():
                    self._merge(oth, sem, val)
        own = e.sem.num
        deps = {}
        for k, (sem, val) in raw.items():
            if k == own and not is_dma and e.name == "pe":
                continue
            self._merge(deps, sem, val)
        for k, (sem, val) in oth.items():
            if k == own and not is_dma:
                continue
            self._merge(deps, sem, val)
        return deps

    def _record(self, tag, reads, writes, pwrites):
        sem, val = tag
        for b in reads:
            self._merge(b.r, sem, val)
        for b in writes:
            b.pw = b.w
            b.pr = b.r
            b.w = {sem.num: (sem, val)}
            b.r = {}
        for b in pwrites:
            self._merge(b.w, sem, val)

    def op(self, en, fn, reads=(), writes=(), pwrites=()):
        e = self.engs[en]
        self._wait(e, self._deps(e, reads, writes, pwrites, False))
        ins = fn(e.eng)
        e.count += 1
        ins.then_inc(e.sem, 1)
        self._record((e.sem, e.count), reads, writes, pwrites)

    def dma(self, q, out, in_, reads=(), writes=(), pwrites=()):
        e = self.engs[q]
        slot = e.slots[e.slot_i % NSLOT]
        e.slot_i += 1
        deps = self._deps(e, reads, writes, pwrites, True)
        if slot[1] > 0:
            self._merge(deps, slot[0], slot[1])
        self._wait(e, deps)
        e.eng.dma_start(out=out, in_=in_).then_inc(slot[0], 16)
        slot[1] += 16
        self._record((slot[0], slot[1]), reads, writes, pwrites)

    def fence(self):
        sp = self.engs["sp"]
        allv = {}
        for e in self.engs.values():
            if e.count > 0:
                self._merge(allv, e.sem, e.count)
            for s in e.slots:
                if s[1] > 0:
                    self._merge(allv, s[0], s[1])
        self._wait(sp, allv)
        sp.eng.dma_start(out=self.fence_d[1:2, 0:8], in_=self.fence_src).then_inc(self.fence_sem, 16)
        self.fence_val += 16
        for e in self.engs.values():
            e.eng.wait_ge(self.fence_sem, self.fence_val)
            for k, (sem, val) in allv.items():
                e.seen[k] = max(e.seen.get(k, 0), val)


def build(S, NTOK, R, last_only=None, dbg=False):
    assert R == 1 and NTOK == S
    HG = H // R
    NT = NTOK // 512
    NKB = S // 128
    nc = bass.Bass("TRN2", target_bir_lowering=False)
    cx = Ctx(nc)
    _uid = [0]

    def nm(n):
        _uid[0] += 1
        return f"{n}_{_uid[0]}"
    op, dma = cx.op, cx.dma

    def din(name, shape, dt=F32):
        return nc.dram_tensor(name, shape, dt, kind="ExternalInput").ap()

    skind = "ExternalOutput" if dbg else "Internal"

    def dscr(name, shape, dt):
        return nc.dram_tensor(name, shape, dt, kind=skind).ap()

    xT = din("xT", [D, NTOK])
    w_in = din("w_in", [L, D, NIN])
    w_a = din("w_a", [L, 512, D])
    w_b = din("w_b", [L, 512, D])
    w_out = din("w_out", [L, D, D])
    w_up = din("w_up", [L, D, DFF])
    w_down = din("w_down", [L, DFF, D])
    n1 = din("n1", [L, 128, 8])
    n2 = din("n2", [L, 128, 8])
    nf = din("nf", [128, 8])
    fb = din("fb", [L, 8, 1])
    tbias = din("tbias", [L, 128, HG * 3 * 128])
    outT = nc.dram_tensor("outT", [D, NTOK], F32, kind="ExternalOutput").ap()
    cx.fence_src = nf[0:1, :]

    XS = dscr("XS", [D, NTOK], F32)
    QA = dscr("QA", [HG * 64, S], BF16)
    QC = dscr("QC", [HG * 64, S], BF16)
    KA = dscr("KA", [HG * 64, S], BF16)
    KC = dscr("KC", [HG * 64, S], BF16)
    VA = dscr("VA", [128, HG, NKB, 64], BF16)
    VC = dscr("VC", [128, HG, NKB, 64], BF16)
    LF = dscr("LF", [HG, S], F32)
    CP = dscr("CP", [HG, 3, S], BF16)
    CN = dscr("CN", [HG, 3, S], BF16)
    G = dscr("G", [2 * D, NTOK], BF16)
    OA = dscr("OA", [512, NTOK], BF16)
    OC = dscr("OC", [512, NTOK], BF16)
    WinB = dscr("WinB", [D, NIN], BF16)
    WupB = dscr("WupB", [D, DFF], BF16)
    WdnB = dscr("WdnB", [DFF, D], BF16)
    WaB = dscr("WaB", [512, D], BF16)
    WbB = dscr("WbB", [512, D], BF16)
    WoB = dscr("WoB", [D, D], BF16)

    def fm(ap):
        return ap.rearrange("(c p) n -> p c n", p=128)

    PSALL = nc.alloc_psum_tensor("psall", [128, 8, 512], F32)
    PS = [PSALL[:, i, :] for i in range(8)]
    PSB = [Buf() for _ in range(8)]
    ones_bf = nc.alloc_sbuf_tensor("ones_bf", [128, 128], BF16)
    tri = nc.alloc_sbuf_tensor("tri", [128, 128], BF16)
    trif = nc.alloc_sbuf_tensor("trif", [128, 128], F32)
    nf_sb = nc.alloc_sbuf_tensor("nf_sb", [128, 8], F32)
    b_const = Buf()
    op("dve", lambda e: e.memset(ones_bf[:], 1.0), writes=[b_const])
    op("pool", lambda e: e.memset(trif[:], 1.0), writes=[b_const])
    op("pool", lambda e: e.affine_select(out=trif[:], in_=trif[:], pattern=[[1, 128]],
                                          compare_op=ALU.is_ge, fill=0.0, base=0,
                                          channel_multiplier=-1), reads=[b_const], writes=[b_const])
    op("dve", lambda e: e.tensor_copy(tri[:], trif[:]), reads=[b_const], writes=[b_const])
    wz = nc.alloc_sbuf_tensor("wz", [128, 512], BF16)
    op("pool", lambda e: e.memset(wz[:], 0.0), writes=[b_const])

    def warm_pe(n, bank, reads=()):
        for i in range(n):
            op("pe", lambda e: e.matmul(PS[bank][:], ident[:], wz[:], start=True, stop=True),
               reads=list(reads), writes=[PSB[bank]] if i == 0 else [], pwrites=[] if i == 0 else [PSB[bank]])
    trineg = nc.alloc_sbuf_tensor("trineg", [128, 128], BF16)
    op("dve", lambda e: e.tensor_scalar(trineg[:], trif[:], -1.0, -NEGM, ALU.add, ALU.mult), reads=[b_const], writes=[b_const])
    ident = nc.alloc_sbuf_tensor("ident", [128, 128], BF16)
    identf = nc.alloc_sbuf_tensor("identf", [128, 128], F32)
    op("pool", lambda e: e.memset(identf[:], 1.0), writes=[b_const])
    op("pool", lambda e: e.affine_select(out=identf[:], in_=identf[:], pattern=[[1, 128]],
                                          compare_op=ALU.is_equal, fill=0.0, base=0,
                                          channel_multiplier=-1), reads=[b_const], writes=[b_const])
    op("dve", lambda e: e.tensor_copy(ident[:], identf[:]), reads=[b_const], writes=[b_const])
    dma("sp", nf_sb[:], nf, writes=[b_const])
    cx.fence()

    def load_weights(dst, src2d, ncols, kchunks, scale_ap, stg, stg_bufs, colstep):
        i = 0
        engs = ("dve", "act", "pool")
        for kc in range(kchunks):
            for c0 in range(0, ncols, colstep):
                c1 = min(ncols, c0 + colstep)
                sb = stg[i % len(stg)]
                bb = stg_bufs[i % len(stg)]
                dma("sp", sb[:, 0:c1 - c0], src2d[kc * 128:(kc + 1) * 128, c0:c1], writes=[bb])
                en = engs[i % 3]
                o_ap = dst[:, kc, c0:c1]
                i_ap = sb[:, 0:c1 - c0]
                if scale_ap is None:
                    if en == "act":
                        op(en, lambda e, o=o_ap, a=i_ap: e.copy(o, a), reads=[bb])
                    else:
                        op(en, (lambda e, o=o_ap, a=i_ap: e.tensor_copy(o, a)) if en == 'dve' else (lambda e, o=o_ap, a=i_ap: e.tensor_scalar(o, a, 1.0, None, ALU.mult)), reads=[bb])
                else:
                    sc = scale_ap[:, kc:kc + 1]
                    if en == "act":
                        op(en, lambda e, o=o_ap, a=i_ap, s=sc: e.activation(o, a, AF.Copy, scale=s), reads=[bb])
                    else:
                        op(en, lambda e, o=o_ap, a=i_ap, s=sc: e.tensor_scalar(o, a, s, None, ALU.mult), reads=[bb])
                i += 1

    def rmsnorm_h(xt_t, xb, sq, sqb, rstd, lnv, nb, hbf, hb, ntok, psi, do_sq=True, sq_eng="pool"):
        if do_sq:
            op(sq_eng, lambda e: e.tensor_tensor(sq[:], xt_t, xt_t, ALU.mult), reads=[xb], writes=[sqb])
        for c in range(8):
            op("pe", lambda e, c=c: e.matmul(PS[psi][:, 0:ntok], ones_bf[:], sq[:, c, :],
                                             start=(c == 0), stop=(c == 7)),
               reads=[sqb], writes=[PSB[psi]] if c == 0 else [], pwrites=[] if c == 0 else [PSB[psi]])
        op("act", lambda e: e.activation(lnv[:, 0:ntok], PS[psi][:, 0:ntok], AF.Ln, bias=EPS, scale=1.0 / D),
           reads=[PSB[psi]], writes=[nb])
        op("act", lambda e: e.activation(rstd[:, 0:ntok], lnv[:, 0:ntok], AF.Exp, scale=-0.5),
           reads=[nb], writes=[nb])
        if hbf is not None:
            for c in range(8):
                en = "dve" if c % 2 == 0 else "pool"
                op(en, lambda e, c=c: e.tensor_tensor(hbf[:, c, :], xt_t[:, c, :], rstd[:, 0:ntok], ALU.mult),
                   reads=[xb, nb], writes=[hb] if c == 0 else [], pwrites=[] if c == 0 else [hb])

    for l in range(L):
        xsrc = xT if l == 0 else XS
        with ExitStack() as es:
            W = es.enter_context(nc.sbuf_tensor(nm("W"), [128, 8, NIN], BF16))
            stgA = [es.enter_context(nc.sbuf_tensor(nm("stg"), [128, 641], F32)) for _ in range(4)]
            g1 = es.enter_context(nc.sbuf_tensor(nm("g1"), [128, 8], F32))
            fbs = es.enter_context(nc.sbuf_tensor(nm("fbs"), [8, 1], F32))
            xt0 = es.enter_context(nc.sbuf_tensor(nm("xt0"), [128, 8, 512], F32))
            xt1 = es.enter_context(nc.sbuf_tensor(nm("xt1"), [128, 8, 512], F32))
            sq0 = es.enter_context(nc.sbuf_tensor(nm("sq0"), [128, 8, 512], BF16))
            sq1 = es.enter_context(nc.sbuf_tensor(nm("sq1"), [128, 8, 512], BF16))
            hbf0 = es.enter_context(nc.sbuf_tensor(nm("hbf0"), [128, 8, 512], BF16))
            hbf1 = es.enter_context(nc.sbuf_tensor(nm("hbf1"), [128, 8, 512], BF16))
            rstd2 = es.enter_context(nc.sbuf_tensor(nm("rstd"), [128, 2, 512], F32))
            lnv2 = es.enter_context(nc.sbuf_tensor(nm("lnv"), [128, 2, 512], F32))
            ev0 = es.enter_context(nc.sbuf_tensor(nm("ev0"), [128, 4, 512], BF16))
            ev1 = es.enter_context(nc.sbuf_tensor(nm("ev1"), [128, 4, 512], BF16))
            ev2 = es.enter_context(nc.sbuf_tensor(nm("ev2"), [128, 4, 512], BF16))
            fz = es.enter_context(nc.sbuf_tensor(nm("fz"), [8, 5, 512], F32))
            xts, xbs = [xt0, xt1], [Buf(), Buf()]
            sqs, sqbs = [sq0, sq1], [Buf(), Buf()]
            hbfs, hbs = [hbf0, hbf1], [Buf(), Buf()]
            nbs = [Buf(), Buf()]
            evs, evb = [ev0, ev1, ev2], [Buf(), Buf(), Buf()]
            fzb, wb_ = Buf(), Buf()
            dma("sp", g1[:], n1[l], writes=[wb_])
            dma("sp", fbs[:], fb[l], writes=[wb_])
            cx.fence()
            if l == 0:
                load_weights(W, w_in[l], NIN, 8, g1, stgA, [Buf() for _ in range(4)], 641)
            else:
                for kc in range(8):
                    dma("sp" if kc % 2 == 0 else "act", W[:, kc, :], WinB[kc * 128:(kc + 1) * 128, :], pwrites=[wb_])
            cx.fence()
            pi = 0
            gi = 0
            groups = [("q", 0, QA), ("q", 512, QC), ("k", 1024, KA), ("k", 1536, KC),
                      ("g", 3072, 0), ("g", 3584, 512), ("g", 4096, 1024), ("g", 4608, 1536),
                      ("v", 2048, VA), ("v", 2560, VC), ("f", 5120, None)]

            def prologue_a(t):
                k = t % 2
                if t + 1 < NT:
                    dma("sp", xts[(t + 1) % 2][:], fm(xsrc)[:, :, (t + 1) * 512:(t + 2) * 512],
                        writes=[xbs[(t + 1) % 2]])
                rmsnorm_h(xts[k][:], xbs[k], sqs[k], sqbs[k], rstd2[:, k, :], lnv2[:, k, :], nbs[k], hbfs[k], hbs[k], 512, 0,
                          do_sq=(t == 0))

            dma("sp", xts[0][:], fm(xsrc)[:, :, 0:512], writes=[xbs[0]])
            prologue_a(0)
            for t in range(NT):
                hbf, hb = hbfs[t % 2], hbs[t % 2]
                tsl = slice(t * 512, (t + 1) * 512)
                for gidx, (kind, c0, dst) in enumerate(groups):
                    if gidx == 0 and t + 1 < NT:
                        k1 = (t + 1) % 2
                        op("dve", lambda e: e.tensor_tensor(sqs[k1][:], xts[k1][:], xts[k1][:], ALU.mult),
                           reads=[xbs[k1]], writes=[sqbs[k1]])
                    if gidx == 6 and t + 1 < NT:
                        prologue_a(t + 1)
                    if kind == "f":
                        for kc in range(8):
                            op("pe", lambda e, kc=kc: e.matmul(PS[7][0:8, :], W[:, kc, 5120:5128], hbf[:, kc, :],
                                                                start=(kc == 0), stop=(kc == 7)),
                               reads=[hb], writes=[PSB[7]] if kc == 0 else [], pwrites=[] if kc == 0 else [PSB[7]])
                        z, a_, e_, m_, lf_ = (fz[:, i, :] for i in range(5))
                        op("dve", lambda e: e.tensor_scalar(z, PS[7][0:8, :], fbs[:, 0:1], None, ALU.add), reads=[PSB[7]], writes=[fzb])
                        op("dve", lambda e: e.tensor_scalar(m_, z, -1.0, None, ALU.mult), reads=[fzb], writes=[fzb])
                        op("dve", lambda e: e.tensor_tensor(a_, z, m_, ALU.max), reads=[fzb], writes=[fzb])
                        op("act", lambda e: e.activation(e_, a_, AF.Exp, scale=-1.0), reads=[fzb], writes=[fzb])
                        op("act", lambda e: e.activation(e_, e_, AF.Ln, bias=1.0, scale=1.0), reads=[fzb], writes=[fzb])
                        op("dve", lambda e: e.tensor_scalar(m_, z, 0.0, None, ALU.min), reads=[fzb], writes=[fzb])
                        op("dve", lambda e: e.tensor_tensor(lf_, m_, e_, ALU.subtract), reads=[fzb], writes=[fzb])
                        dma("pool", LF[:, tsl], lf_, reads=[fzb])
                        continue
                    ev, eb = evs[gi % 3], evb[gi % 3]
                    gi += 1
                    for j in range(4):
                        p = 1 + pi % 6
                        pi += 1
                        for kc in range(8):
                            if kind == "v":
                                op("pe", lambda e, kc=kc: e.matmul(
                                    PS[p][:], hbf[:, kc, j * 128:(j + 1) * 128], W[:, kc, c0:c0 + 512],
                                    start=(kc == 0), stop=(kc == 7)),
                                   reads=[hb], writes=[PSB[p]] if kc == 0 else [], pwrites=[] if kc == 0 else [PSB[p]])
                            else:
                                cc = c0 + j * 128
                                op("pe", lambda e, kc=kc: e.matmul(
                                    PS[p][:], W[:, kc, cc:cc + 128], hbf[:, kc, :], start=(kc == 0), stop=(kc == 7)),
                                   reads=[hb], writes=[PSB[p]] if kc == 0 else [], pwrites=[] if kc == 0 else [PSB[p]])
                        wr = dict(writes=[eb]) if j == 0 else dict(pwrites=[eb])
                        if kind == "q":
                            op("dve", lambda e: e.tensor_scalar(ev[:, j, :], PS[p][:], 0.125, None, ALU.mult), reads=[PSB[p]], **wr)
                        elif kind == "k":
                            op("dve", lambda e: e.tensor_copy(ev[:, j, :], PS[p][:]), reads=[PSB[p]], **wr)
                        elif kind == "g":
                            op("act", lambda e: e.activation(ev[:, j, :], PS[p][:], AF.Sigmoid), reads=[PSB[p]], **wr)
                        else:
                            ev4 = ev[:].rearrange("p a b -> p (a b)").rearrange("p (h t d) -> p h t d", h=8, t=4)
                            src3 = PS[p][:].rearrange("p (h d) -> p h d", d=64)
                            if j % 2 == 0:
                                op("dve", lambda e: e.tensor_copy(ev4[:, :, j, :], src3), reads=[PSB[p]], **wr)
                            else:
                                op("act", lambda e: e.copy(ev4[:, :, j, :], src3), reads=[PSB[p]], **wr)
                    if kind == "g":
                        dma("pool", fm(G)[:, dst // 128:dst // 128 + 4, tsl], ev[:], reads=[eb])
                    elif kind == "v":
                        ev4 = ev[:].rearrange("p a b -> p (a b)").rearrange("p (h t d) -> p h t d", h=8, t=4)
                        dma("pool", dst[:, :, t * 4:(t + 1) * 4, :], ev4, reads=[eb])
                    else:
                        dma("pool", fm(dst)[:, :, tsl], ev[:], reads=[eb])
            cx.fence()
        if last_only == "A":
            break
        with ExitStack() as es:
            T0 = es.enter_context(nc.sbuf_tensor(nm("T0"), [8, S], F32))
            T1 = es.enter_context(nc.sbuf_tensor(nm("T1"), [8, S], F32))
            P3 = es.enter_context(nc.sbuf_tensor(nm("P3"), [8, 3, S], BF16))
            tb_ = Buf()
            dma("sp", T0[:], LF, writes=[tb_])
            op("dve", lambda e: e.tensor_tensor_scan(T1[:], T0[:], T0[:], 0.0, ALU.add, ALU.min), reads=[tb_], writes=[tb_])
            op("dve", lambda e: e.tensor_copy(P3[:, 0, :], T1[:]), reads=[tb_], writes=[tb_])
            op("dve", lambda e: e.tensor_tensor(T0[:], T1[:], P3[:, 0, :], ALU.subtract), reads=[tb_], writes=[tb_])
            op("dve", lambda e: e.tensor_copy(P3[:, 1, :], T0[:]), reads=[tb_], writes=[tb_])
            op("dve", lambda e: e.tensor_tensor(T1[:], T0[:], P3[:, 1, :], ALU.subtract), reads=[tb_], writes=[tb_])
            op("dve", lambda e: e.tensor_copy(P3[:, 2, :], T1[:]), reads=[tb_], writes=[tb_])
            dma("sp", CP, P3[:], reads=[tb_])
            cx.fence()
            op("dve", lambda e: e.tensor_scalar(P3[:], P3[:], -1.0, None, ALU.mult), writes=[tb_])
            dma("sp", CN, P3[:], reads=[tb_])
            cx.fence()
        with ExitStack() as es:
            KT0 = es.enter_context(nc.sbuf_tensor(nm("KT0"), [70, S], BF16))
            KT1 = es.enter_context(nc.sbuf_tensor(nm("KT1"), [70, S], BF16))
            QT0 = es.enter_context(nc.sbuf_tensor(nm("QT0"), [70, S], BF16))
            QT1 = es.enter_context(nc.sbuf_tensor(nm("QT1"), [70, S], BF16))
            VT0 = es.enter_context(nc.sbuf_tensor(nm("VT0"), [128, NKB, 128], BF16))
            VT1 = es.enter_context(nc.sbuf_tensor(nm("VT1"), [128, NKB, 128], BF16))
            PT = es.enter_context(nc.sbuf_tensor(nm("PT"), [128, 6, 512], BF16))
            BSh = es.enter_context(nc.sbuf_tensor(nm("BSh"), [128, HG, 5, 128], BF16))
            BSl = es.enter_context(nc.sbuf_tensor(nm("BSl"), [128, HG, 5, 128], BF16))
            Eo = es.enter_context(nc.sbuf_tensor(nm("Eo"), [128, 2, 512], F32))
            Dn = es.enter_context(nc.sbuf_tensor(nm("Dn"), [64, 2, 512], F32))
            On = es.enter_context(nc.sbuf_tensor(nm("On"), [64, 2, 512], BF16))
            BS = es.enter_context(nc.sbuf_tensor(nm("BS"), [128, HG, 5, 128], F32))
            KT, QT, VT = [KT0, KT1], [QT0, QT1], [VT0, VT1]
            kvb = [Buf(), Buf()]
            ptb = [Buf() for _ in range(3)]
            ptc = [Buf() for _ in range(6)]
            sfb = [Buf() for _ in range(4)]
            eob, dnb, onb = [Buf(), Buf()], [Buf(), Buf()], [Buf(), Buf()]
            cb = Buf()
            tbv = tbias[l].rearrange("p (h k t) -> p h k t", h=HG, k=3)
            dma("sp", BS[:, :, 0:3, :], tbv, writes=[cb])
            dma("sp", BS[:, :, 3, :], tbv[:, :, 2, :], pwrites=[cb])
            dma("sp", BS[:, :, 4, :], tbv[:, :, 2, :], pwrites=[cb])
            for b in range(2):
                op("dve", lambda e, b=b: e.memset(KT[b][64:70, :], 1.0), writes=[cb])
                op("pool", lambda e, b=b: e.memset(QT[b][64:70, :], 1.0), writes=[cb])
                op("pool", lambda e, b=b: e.memset(VT[b][:, :, 64:128], 1.0), writes=[cb])
            cx.fence()
            op("pool", lambda e: e.memset(BS[64:128, :, 0, 0:64], NEGM), writes=[cb])
            op("pool", lambda e: e.memset(BS[0:64, :, 4, 64:128], NEGM), writes=[cb])
            cx.fence()
            fl = "p h k t -> p (h k t)"
            op("dve", lambda e: e.tensor_copy(BSh[:].rearrange(fl), BS[:].rearrange(fl)), writes=[cb])
            cx.fence()
            op("dve", lambda e: e.tensor_tensor(BS[:].rearrange(fl), BS[:].rearrange(fl), BSh[:].rearrange(fl), ALU.subtract), writes=[cb])
            cx.fence()
            op("dve", lambda e: e.tensor_copy(BSl[:].rearrange(fl), BS[:].rearrange(fl)), writes=[cb])
            cx.fence()
            oi = 0

            def load_head(b, Ksrc, Qsrc, Vsrc, hl, fox):
                rs = slice(hl * 64, (hl + 1) * 64)
                dma("sp", KT[b][0:64, :], Ksrc[rs, :], pwrites=[kvb[b]])
                dma("sp", QT[b][0:64, :], Qsrc[rs, :], pwrites=[kvb[b]])
                if fox:
                    dma("sp", KT[b][64:67, :], CN[hl], pwrites=[kvb[b]])
                    dma("sp", QT[b][67:70, :], CP[hl], pwrites=[kvb[b]])
                vsrc = Vsrc[:, hl, :, :]
                half = NKB // 2
                dma("sp", VT[b][:, 0:half, 0:64], vsrc[:, 0:half, :], pwrites=[kvb[b]])
                dma("sp", VT[b][:, half:NKB, 0:64], vsrc[:, half:NKB, :], pwrites=[kvb[b]])

            pending = []
            pcs = [es.enter_context(nc.sbuf_tensor(nm("pcs"), [128, 2048], F32)) for _ in range(2)]
            pcd = [es.enter_context(nc.sbuf_tensor(nm("pcd"), [128, 2048], BF16)) for _ in range(2)]
            gp1 = es.enter_context(nc.sbuf_tensor(nm("gp1"), [128, 8], F32))
            gp2 = es.enter_context(nc.sbuf_tensor(nm("gp2"), [128, 8], F32))
            pcsb, pcdb, gpb = [Buf(), Buf()], [Buf(), Buf()], Buf()
            dma("sp", gp2[:], n2[l], writes=[gpb])
            if l + 1 < L:
                dma("sp", gp1[:], n1[l + 1], pwrites=[gpb])
            pc_steps = []
            jobs = [(w_a[l], WaB, D, 4, None), (w_b[l], WbB, D, 4, None), (w_out[l], WoB, D, 8, None),
                    (w_up[l], WupB, DFF, 8, gp2), (w_down[l], WdnB, D, 32, None)]
            if l + 1 < L:
                jobs.append((w_in[l + 1], WinB, NIN, 8, gp1))
            for src2d, dst2d, ncols, kch, sc in jobs:
                for kc in range(kch):
                    for c0 in range(0, ncols, 2048):
                        pc_steps.append((src2d, dst2d, kc, c0, min(ncols, c0 + 2048), sc))
            pc_i = [0]
            pc_prev = []

            def precast_step():
                if pc_i[0] >= len(pc_steps):
                    return
                i = pc_i[0]
                pc_i[0] += 1
                src2d, dst2d, kc, c0, c1, sc = pc_steps[i]
                k = i % 2
                w = c1 - c0
                rows = slice(kc * 128, (kc + 1) * 128)
                if pc_prev:
                    pd, pk, pw, prow, pc0, pc1 = pc_prev.pop()
                    dma("sp", pd[prow, pc0:pc1], pcd[pk][:, 0:pw], reads=[pcdb[pk]])
                dma("sp", pcs[k][:, 0:w], src2d[rows, c0:c1], writes=[pcsb[k]])
                if sc is None:
                    op("dve", lambda e: e.tensor_copy(pcd[k][:, 0:w], pcs[k][:, 0:w]),
                       reads=[pcsb[k]], writes=[pcdb[k]])
                else:
                    op("dve", lambda e: e.tensor_scalar(pcd[k][:, 0:w], pcs[k][:, 0:w], sc[:, kc:kc + 1], None, ALU.mult),
                       reads=[pcsb[k], gpb], writes=[pcdb[k]])
                pc_prev.append((dst2d, k, w, rows, c0, c1))

            def epilogue(o, psb, dst, hl, T, copy_eng="dve"):
                precast_step()
                if copy_eng == "dve":
                    op("dve", lambda e: e.tensor_copy(Eo[:, o, :], PS[psb][:]), reads=[PSB[psb]], writes=[eob[o]])
                else:
                    op("act", lambda e: e.copy(Eo[:, o, :], PS[psb][:]), reads=[PSB[psb]], writes=[eob[o]])
                dma("sp", Dn[:, o, :], Eo[64:128, o, :], reads=[eob[o]], writes=[dnb[o]])

                def part2():
                    if dst is OC:
                        op("act", lambda e: e.activation(Dn[:, o, :], Dn[:, o, :], AF.Ln), reads=[dnb[o]], writes=[dnb[o]])
                        op("act", lambda e: e.activation(Dn[:, o, :], Dn[:, o, :], AF.Exp, scale=-1.0), reads=[dnb[o]], writes=[dnb[o]])
                    else:
                        op("dve", lambda e: e.reciprocal(Dn[:, o, :], Dn[:, o, :]), reads=[dnb[o]], writes=[dnb[o]])
                    op("dve", lambda e: e.tensor_tensor(On[:, o, :], Eo[0:64, o, :], Dn[:, o, :], ALU.mult),
                       reads=[eob[o], dnb[o]], writes=[onb[o]])
                    dma("sp", dst[hl * 64:(hl + 1) * 64, T * 512:(T + 1) * 512], On[:, o, :], reads=[onb[o]])
                pending.append([4, part2])

            def tick(flush=False):
                for it in list(pending):
                    it[0] -= 1
                    if flush or it[0] <= 0:
                        it[1]()
                        pending.remove(it)

            PAIRB = [0, 2, 6]
            heads = [("fox", hl) for hl in range(HG)] + [("chk", hl) for hl in range(HG)]
            def load_head_idx(hi):
                kind_, hl_ = heads[hi]
                if kind_ == "fox":
                    load_head(hi % 2, KA, QA, VA, hl_, True)
                else:
                    load_head(hi % 2, KC, QC, VC, hl_, False)

            load_head_idx(0)
            for hidx, (kind, hl) in enumerate(heads):
                b = hidx % 2
                fox = kind == "fox"
                if hidx + 1 < len(heads):
                    load_head_idx(hidx + 1)
                if fox:
                    units = []
                    t_order = []
                    lo_, hi_t = 0, NT - 1
                    while lo_ <= hi_t:
                        t_order.append(hi_t)
                        if lo_ != hi_t:
                            t_order.append(lo_)
                        lo_ += 1
                        hi_t -= 1
                    for T in t_order:
                        o = oi % 2
                        oi += 1
                        tl = []
                        for kb in range(0, 4 * T, 2):
                            tl.append(dict(T=T, o=o, diag=False, blocks=[(kb, 0), (kb + 1, 0)]))
                        tl.append(dict(T=T, o=o, diag=True, blocks=[(4 * T, 0), (4 * T + 1, 128)]))
                        tl.append(dict(T=T, o=o, diag=True, blocks=[(4 * T + 2, 256), (4 * T + 3, 384)]))
                        tl[0]["first"] = True
                        tl[-1]["last"] = True
                        units += tl

                    def f_qk(ui):
                        u = units[ui]
                        pb = PAIRB[ui % 3]
                        T = u["T"]
                        for jj, (kb, q0) in enumerate(u["blocks"]):
                            dg = u["diag"]
                            op("pe", lambda e: e.matmul(PS[pb + jj][:, q0:512], KT[b][0:70, kb * 128:(kb + 1) * 128],
                                                        QT[b][0:70, T * 512 + q0:(T + 1) * 512], start=True, stop=not dg),
                               reads=[kvb[b]], writes=[PSB[pb + jj]])
                            if dg:
                                op("pe", lambda e: e.matmul(PS[pb + jj][:, q0:q0 + 128], ident[:], trineg[:], start=False, stop=True),
                                   pwrites=[PSB[pb + jj]])

                    def f_exp(ui):
                        u = units[ui]
                        pb = PAIRB[ui % 3]
                        pi_ = ui % 3
                        if not u["diag"]:
                            op("act", lambda e: e.activation(PT[:, 2 * pi_:2 * pi_ + 2, :], PSALL[:, pb:pb + 2, :], AF.Exp),
                               reads=[PSB[pb], PSB[pb + 1]], writes=[ptb[pi_]])
                        else:
                            for jj, (kb, q0) in enumerate(u["blocks"]):
                                wr = dict(writes=[ptb[pi_]]) if jj == 0 else dict(pwrites=[ptb[pi_]])
                                op("act", lambda e: e.activation(PT[:, 2 * pi_ + jj, q0:512], PS[pb + jj][:, q0:512], AF.Exp),
                                   reads=[PSB[pb + jj]], **wr)

                    def f_pv(ui):
                        u = units[ui]
                        pi_ = ui % 3
                        psb = 4 + u["o"]
                        for jj, (kb, q0) in enumerate(u["blocks"]):
                            st = bool(u.get("first")) and jj == 0
                            sp_ = bool(u.get("last")) and jj == 1
                            op("pe", lambda e: e.matmul(PS[psb][:, q0:512], VT[b][:, kb, :], PT[:, 2 * pi_ + jj, q0:512],
                                                        start=st, stop=sp_),
                               reads=[ptb[pi_], kvb[b]], writes=[PSB[psb]] if st else [], pwrites=[] if st else [PSB[psb]])
                        if u.get("last"):
                            epilogue(u["o"], psb, OA, hl, u["T"])

                    LOOK = 2
                    for ui in range(min(LOOK, len(units))):
                        f_qk(ui)
                    for ui in range(len(units)):
                        if ui + LOOK < len(units):
                            f_qk(ui + LOOK)
                        f_exp(ui)
                        f_pv(ui)
                        tick()
                    tick(True)
                else:
                    if hl == 0:
                        cx.fence()
                    units = []
                    for T in range(NT):
                        o = oi % 2
                        oi += 1
                        i0 = 4 * T
                        tl = []
                        for j in [i0, i0 - 1, i0 - 2, i0 - 3, i0 - 4, i0 + 1, i0 + 2, i0 + 3]:
                            if j < 0:
                                continue
                            a = max(j, i0)
                            bq = min(j + 4, i0 + 3)
                            tl.append(dict(T=T, o=o, j=j, col0=(a - i0) * 128, n=(bq - a + 1) * 128, dk0=a - j, dk1=bq - j + 1))
                        tl[0]["first"] = True
                        tl[-1]["last"] = True
                        units += tl

                    SBK = [0, 1, 2, 3, 6, 7]

                    def c_qk(ui):
                        u = units[ui]
                        s = SBK[ui % 6]
                        j, n = u["j"], u["n"]
                        qc0 = u["T"] * 512 + u["col0"]
                        bh = BSh[:, hl, u["dk0"]:u["dk1"], :].rearrange("p a t -> p (a t)")
                        bl = BSl[:, hl, u["dk0"]:u["dk1"], :].rearrange("p a t -> p (a t)")
                        op("pe", lambda e: e.matmul(PS[s][:, 0:n], KT[b][0:64, j * 128:(j + 1) * 128], QT[b][0:64, qc0:qc0 + n],
                                                    start=True, stop=False), reads=[kvb[b]], writes=[PSB[s]])
                        op("pe", lambda e: e.matmul(PS[s][:, 0:n], ident[:], bh, start=False, stop=False), pwrites=[PSB[s]])
                        op("pe", lambda e: e.matmul(PS[s][:, 0:n], ident[:], bl, start=False, stop=True), pwrites=[PSB[s]])

                    def c_rest(ui):
                        u = units[ui]
                        s = SBK[ui % 6]
                        r = ui % 6
                        j, n = u["j"], u["n"]
                        psb = 4 + u["o"]
                        op("act", lambda e: e.activation(PT[:, r, 0:n], PS[s][:, 0:n], AF.Exp), reads=[PSB[s]], writes=[ptc[r]])
                        st = bool(u.get("first"))
                        op("pe", lambda e: e.matmul(PS[psb][:, u["col0"]:u["col0"] + n], VT[b][:, j, :], PT[:, r, 0:n],
                                                    start=st, stop=bool(u.get("last"))),
                           reads=[ptc[r], kvb[b]], writes=[PSB[psb]] if st else [], pwrites=[] if st else [PSB[psb]])
                        if u.get("last"):
                            epilogue(u["o"], psb, OC, hl, u["T"], "dve")

                    LOOK = 5
                    for ui in range(min(LOOK, len(units))):
                        c_qk(ui)
                    for ui in range(len(units)):
                        if ui + LOOK < len(units):
                            c_qk(ui + LOOK)
                        c_rest(ui)
                        tick()
                    tick(True)
            while pc_i[0] < len(pc_steps):
                precast_step()
            if pc_prev:
                pd, pk, pw, prow, pc0, pc1 = pc_prev.pop()
                dma("sp", pd[prow, pc0:pc1], pcd[pk][:, 0:pw], reads=[pcdb[pk]])
            cx.fence()
        if last_only == "B":
            break
        with ExitStack() as es:
            Wa = es.enter_context(nc.sbuf_tensor(nm("Wa"), [128, 4, D], BF16))
            Wb = es.enter_context(nc.sbuf_tensor(nm("Wb"), [128, 4, D], BF16))
            Wo = es.enter_context(nc.sbuf_tensor(nm("Wo"), [128, 8, D], BF16))
            stgC = [es.enter_context(nc.sbuf_tensor(nm("stg"), [128, 512], F32)) for _ in range(4)]
            xt0 = es.enter_context(nc.sbuf_tensor(nm("xt0"), [128, 8, 512], F32))
            xt1 = es.enter_context(nc.sbuf_tensor(nm("xt1"), [128, 8, 512], F32))
            ot0 = es.enter_context(nc.sbuf_tensor(nm("ot0"), [128, 8, 512], BF16))
            ot1 = es.enter_context(nc.sbuf_tensor(nm("ot1"), [128, 8, 512], BF16))
            gt0 = es.enter_context(nc.sbuf_tensor(nm("gt0"), [128, 16, 512], BF16))
            gt1 = es.enter_context(nc.sbuf_tensor(nm("gt1"), [128, 16, 512], BF16))
            mt0 = es.enter_context(nc.sbuf_tensor(nm("mt0"), [128, 8, 512], BF16))
            mt1 = es.enter_context(nc.sbuf_tensor(nm("mt1"), [128, 8, 512], BF16))
            t1 = es.enter_context(nc.sbuf_tensor(nm("t1"), [128, 2, 512], F32))
            t2 = es.enter_context(nc.sbuf_tensor(nm("t2"), [128, 2, 512], F32))
            stgs, stb = stgC, [Buf() for _ in range(4)]
            wlb = Buf()
            dma("sp", Wa[:], fm(WaB), pwrites=[wlb])
            dma("act", Wb[:], fm(WbB), pwrites=[wlb])
            dma("sp", Wo[:, 0:4, :], fm(WoB)[:, 0:4, :], pwrites=[wlb])
            dma("act", Wo[:, 4:8, :], fm(WoB)[:, 4:8, :], pwrites=[wlb])
            cx.fence()
            xts, xbs = [xt0, xt1], [Buf(), Buf()]
            ots, obs = [ot0, ot1], [Buf(), Buf()]
            gts, gbs = [gt0, gt1], [Buf(), Buf()]
            mts, mbs = [mt0, mt1], [Buf(), Buf()]
            tbs = [Buf(), Buf()]
            pi = 0

            def loads(t):
                tsl = slice(t * 512, (t + 1) * 512)
                k = t % 2
                dma("sp", ots[k][:, 0:4, :], fm(OA)[:, :, tsl], pwrites=[obs[k]])
                dma("sp", ots[k][:, 4:8, :], fm(OC)[:, :, tsl], pwrites=[obs[k]])
                dma("sp", gts[k][:, 0:8, :], fm(G)[:, 0:8, tsl], pwrites=[gbs[k]])
                dma("sp", gts[k][:, 8:16, :], fm(G)[:, 8:16, tsl], pwrites=[gbs[k]])
                dma("act", xts[k][:], fm(xsrc)[:, :, tsl], writes=[xbs[k]])

            def merge_part(t):
                nonlocal pi
                k = t % 2
                ot, gt, mt, mb = ots[k], gts[k], mts[k], mbs[k]
                for oc in range(8):
                    pa = pi % 8
                    pb = (pi + 1) % 8
                    pi += 2
                    for kc in range(4):
                        op("pe", lambda e, kc=kc: e.matmul(PS[pa][:], Wa[:, kc, oc * 128:(oc + 1) * 128], ot[:, kc, :],
                                                           start=(kc == 0), stop=(kc == 3)),
                           reads=[obs[k]], writes=[PSB[pa]] if kc == 0 else [], pwrites=[] if kc == 0 else [PSB[pa]])
                    for kc in range(4):
                        op("pe", lambda e, kc=kc: e.matmul(PS[pb][:], Wb[:, kc, oc * 128:(oc + 1) * 128], ot[:, 4 + kc, :],
                                                           start=(kc == 0), stop=(kc == 3)),
                           reads=[obs[k]], writes=[PSB[pb]] if kc == 0 else [], pwrites=[] if kc == 0 else [PSB[pb]])
                    tk = oc % 2
                    op("dve", lambda e: e.tensor_tensor(t1[:, tk, :], PS[pa][:], gt[:, oc, :], ALU.mult),
                       reads=[PSB[pa], gbs[k]], writes=[tbs[tk]])
                    op("dve", lambda e: e.tensor_tensor(t2[:, tk, :], PS[pb][:], gt[:, 8 + oc, :], ALU.mult),
                       reads=[PSB[pb], gbs[k]], pwrites=[tbs[tk]])
                    op("pool", lambda e: e.tensor_tensor(mt[:, oc, :], t1[:, tk, :], t2[:, tk, :], ALU.add),
                       reads=[tbs[tk]], writes=[mb] if oc == 0 else [], pwrites=[] if oc == 0 else [mb])

            def out_part(t):
                nonlocal pi
                k = t % 2
                xt_t, mt, mb = xts[k], mts[k], mbs[k]
                for oc in range(8):
                    py = pi % 8
                    pi += 1
                    for kc in range(8):
                        op("pe", lambda e, kc=kc: e.matmul(PS[py][:], Wo[:, kc, oc * 128:(oc + 1) * 128], mt[:, kc, :],
                                                           start=(kc == 0), stop=(kc == 7)),
                           reads=[mb], writes=[PSB[py]] if kc == 0 else [], pwrites=[] if kc == 0 else [PSB[py]])
                    op("dve", lambda e: e.tensor_tensor(xt_t[:, oc, :], xt_t[:, oc, :], PS[py][:], ALU.add),
                       reads=[PSB[py]], pwrites=[xbs[k]])
                dma("pool", fm(XS)[:, :, t * 512:(t + 1) * 512], xt_t[:], reads=[xbs[k]])

            loads(0)
            merge_part(0)
            for t in range(NT):
                if t + 1 < NT:
                    loads(t + 1)
                    merge_part(t + 1)
                out_part(t)
            cx.fence()
        if last_only == "C1":
            break
        TT = 256
        NT2 = NTOK // TT
        with ExitStack() as es:
            Wu = es.enter_context(nc.sbuf_tensor(nm("Wu"), [128, 8, DFF], BF16))
            Wd = es.enter_context(nc.sbuf_tensor(nm("Wd"), [128, 32, D], BF16))
            g2 = es.enter_context(nc.sbuf_tensor(nm("g2"), [128, 8], F32))
            with ExitStack() as es:
                gb_ = Buf()
                dma("sp", g2[:], n2[l], writes=[gb_])
                cx.fence()
                for kc in range(8):
                    dma("sp" if kc % 2 == 0 else "act", Wu[:, kc, :], WupB[kc * 128:(kc + 1) * 128, :], pwrites=[gb_])
                for q4 in range(4):
                    dma("sp" if q4 % 2 == 0 else "act", Wd[:, q4 * 8:(q4 + 1) * 8, :], fm(WdnB)[:, q4 * 8:(q4 + 1) * 8, :], pwrites=[gb_])
                cx.fence()
            with ExitStack() as es:
                xt0 = es.enter_context(nc.sbuf_tensor(nm("xt0"), [128, 8, TT], F32))
                xt1 = es.enter_context(nc.sbuf_tensor(nm("xt1"), [128, 8, TT], F32))
                sq0 = es.enter_context(nc.sbuf_tensor(nm("sq0"), [128, 8, TT], BF16))
                sq1 = es.enter_context(nc.sbuf_tensor(nm("sq1"), [128, 8, TT], BF16))
                hbf0 = es.enter_context(nc.sbuf_tensor(nm("hbf0"), [128, 8, TT], BF16))
                hbf1 = es.enter_context(nc.sbuf_tensor(nm("hbf1"), [128, 8, TT], BF16))
                rstd2 = es.enter_context(nc.sbuf_tensor(nm("rstd"), [128, 2, TT], F32))
                lnv2 = es.enter_context(nc.sbuf_tensor(nm("lnv"), [128, 2, TT], F32))
                u = es.enter_context(nc.sbuf_tensor(nm("u"), [128, 32, TT], BF16))
                rr = es.enter_context(nc.sbuf_tensor(nm("rr"), [128, 2, TT], F32))
                xts, xbs = [xt0, xt1], [Buf(), Buf()]
                sqs, sqbs = [sq0, sq1], [Buf(), Buf()]
                hbfs, hbs = [hbf0, hbf1], [Buf(), Buf()]
                nbs = [Buf(), Buf()]
                ub = Buf()
                rrb = [Buf() for _ in range(2)]
                pi = 0
                ri = 0

                def prologue_c(t):
                    k = t % 2
                    rmsnorm_h(xts[k][:], xbs[k], sqs[k], sqbs[k], rstd2[:, k, :], lnv2[:, k, :], nbs[k], hbfs[k], hbs[k], TT, 0)

                dma("sp", xts[0][:], fm(XS)[:, :, 0:TT], writes=[xbs[0]])
                prologue_c(0)
                for t in range(NT2):
                    k = t % 2
                    xt_t, xb, hbf, hb = xts[k], xbs[k], hbfs[k], hbs[k]
                    if t + 1 < NT2:
                        dma("sp", xts[(t + 1) % 2][:], fm(XS)[:, :, (t + 1) * TT:(t + 2) * TT], writes=[xbs[(t + 1) % 2]])
                    for oc in range(32):
                        p = 1 + pi % 5
                        pi += 1
                        for kc in range(8):
                            op("pe", lambda e, kc=kc: e.matmul(PS[p][:, 0:TT], Wu[:, kc, oc * 128:(oc + 1) * 128], hbf[:, kc, :],
                                                               start=(kc == 0), stop=(kc == 7)),
                               reads=[hb], writes=[PSB[p]] if kc == 0 else [], pwrites=[] if kc == 0 else [PSB[p]])
                        r = ri % 2
                        ri += 1
                        op("act", lambda e: e.activation(rr[:, r, :], PS[p][:, 0:TT], AF.Relu), reads=[PSB[p]], writes=[rrb[r]])
                        op("dve", lambda e: e.tensor_tensor(u[:, oc, :], rr[:, r, :], rr[:, r, :], ALU.mult),
                           reads=[rrb[r]], writes=[ub] if oc == 0 else [], pwrites=[] if oc == 0 else [ub])
                    if t + 1 < NT2:
                        prologue_c(t + 1)
                    for oc in range(8):
                        p = 6 + pi % 2
                        pi += 1
                        for kc in range(32):
                            op("pe", lambda e, kc=kc: e.matmul(PS[p][:, 0:TT], Wd[:, kc, oc * 128:(oc + 1) * 128], u[:, kc, :],
                                                               start=(kc == 0), stop=(kc == 31)),
                               reads=[ub], writes=[PSB[p]] if kc == 0 else [], pwrites=[] if kc == 0 else [PSB[p]])
                        op("dve", lambda e: e.tensor_tensor(xt_t[:, oc, :], xt_t[:, oc, :], PS[p][:, 0:TT], ALU.add),
                           reads=[PSB[p]], pwrites=[xb])
                    if l == L - 1:
                        rmsnorm_h(xt_t[:], xb, sqs[k], sqbs[k], rstd2[:, k, :], lnv2[:, k, :], nbs[k], None, None, TT, 0)
                        for c in range(8):
                            op("dve", lambda e, c=c: e.scalar_tensor_tensor(xt_t[:, c, :], xt_t[:, c, :], nf_sb[:, c:c + 1], rstd2[:, k, :],
                                                                          ALU.mult, ALU.mult),
                               reads=[nbs[k]], pwrites=[xb])
                        dma("pool", fm(outT)[:, :, t * TT:(t + 1) * TT], xt_t[:], reads=[xb])
                    else:
                        dma("pool", fm(XS)[:, :, t * TT:(t + 1) * TT], xt_t[:], reads=[xb])
                cx.fence()
    cx.fence()
    return nc


def prep_weights(inp):
    w = np.asarray(inp["w_in"], np.float32)
    perm = np.concatenate([np.arange(0, 512), np.arange(1544, 2056), np.arange(512, 1024), np.arange(2056, 2568),
                           np.arange(1024, 1536), np.arange(2568, 3080), np.arange(3080, 5128), np.arange(1536, 1544)])
    w_in_p = np.ascontiguousarray(w[:, :, perm])

    def pc(v):
        v = np.asarray(v, np.float32)
        return np.ascontiguousarray(np.swapaxes(v.reshape(v.shape[:-1] + (8, 128)), -1, -2))
    rb = np.asarray(inp["rel_bias"], np.float32)
    sp = np.arange(128)[:, None]
    tp = np.arange(128)[None, :]
    tiles = []
    for k in range(3):
        idx = np.clip(128 * k + tp - sp, -128, 128) + 128
        tiles.append(rb[:, :, idx])
    tb = np.stack(tiles, axis=2)
    tb = np.ascontiguousarray(tb.transpose(0, 3, 1, 2, 4)).reshape(L, 128, H * 3 * 128)
    return dict(
        w_in=w_in_p,
        w_a=np.ascontiguousarray(inp["w_branch_a"], np.float32),
        w_b=np.ascontiguousarray(inp["w_branch_b"], np.float32),
        w_out=np.ascontiguousarray(inp["w_out"], np.float32),
        w_up=np.ascontiguousarray(inp["w_up"], np.float32),
        w_down=np.ascontiguousarray(inp["w_down"], np.float32),
        n1=pc(inp["norm1"]), n2=pc(inp["norm2"]), nf=pc(inp["final_norm"]),
        fb=np.ascontiguousarray(np.asarray(inp["forget_bias"], np.float32)[:, :, None]),
        tbias=tb,
    )


def kernel(**inp):
    x = np.asarray(inp["x"], np.float32)
    B, S, _ = x.shape
    wts = prep_weights(inp)
    zw = {k: np.zeros_like(v) for k, v in wts.items()}
    nc = build(S, S, 1)
    active = {2 * b: b for b in range(B)}
    in_maps = []
    for c in range(8):
        if c in active:
            m = dict(wts)
            m["xT"] = np.ascontiguousarray(x[active[c]].T)
        else:
            m = dict(zw)
            m["xT"] = np.zeros((D, S), np.float32)
        in_maps.append(m)
    res = run_bass_kernel_spmd(nc, in_maps, core_ids=list(range(8)))
    out = np.stack([np.ascontiguousarray(res.results[2 * b]["outT"].T) for b in range(B)], axis=0)
    return out.astype(np.float32)
```
